# Optimizing a Trainium2 kernel written in Bass

```python
import jax, jax.numpy as jnp
from jax import lax
import numpy as np

D_MODEL = 1024
BATCH = 16
SEQ = 256
DEPTH = 1
DEC_BATCH = 8
DEC_SEQ = 1024
PAST_LEN = 256

GRID_W = 64
HEAD_DIM = 64
ATTN_DIM = D_MODEL // 2
N_HEADS = ATTN_DIM // HEAD_DIM
N_KV = 2
KV_GROUP = N_HEADS // N_KV
KV_DIM = N_KV * HEAD_DIM
CONV_DIM = D_MODEL - ATTN_DIM
CONV_WIDTH = 3
D_FF = -(-(8 * D_MODEL) // (3 * 256)) * 256
IN_DIM = ATTN_DIM + 2 * KV_DIM + 3 * CONV_DIM
ROT_PAIRS = HEAD_DIM // 4
ROPE_THETA = 10000.0
Q_BLOCK = 128
RMS_EPS = 1e-6

kernel_name = "hymba_diffusion_prefix_step"


def _rms(x, g):
    xf = x.astype(jnp.float32)
    y = xf * lax.rsqrt(jnp.mean(xf * xf, axis=-1, keepdims=True) + RMS_EPS)
    return (y * g.astype(jnp.float32)).astype(x.dtype)


def _axial_rope_tables(n_tokens, dtype):
    rows = n_tokens // GRID_W
    row = jnp.repeat(jnp.arange(rows, dtype=jnp.float32), GRID_W)
    col = jnp.tile(jnp.arange(GRID_W, dtype=jnp.float32), rows)
    inv = 1.0 / (ROPE_THETA ** (jnp.arange(ROT_PAIRS, dtype=jnp.float32) / ROT_PAIRS))
    ang = jnp.stack([row[:, None] * inv, col[:, None] * inv], axis=1)
    return jnp.cos(ang).astype(dtype), jnp.sin(ang).astype(dtype)


def _apply_rope(x, cos, sin):
    b, s, h, _ = x.shape
    xr = x.reshape(b, s, h, 2, 2, ROT_PAIRS)
    x1, x2 = xr[..., 0, :], xr[..., 1, :]
    c = cos[None, :, None]
    sn = sin[None, :, None]
    out = jnp.stack([x1 * c - x2 * sn, x1 * sn + x2 * c], axis=-2)
    return out.reshape(b, s, h, HEAD_DIM)


def _block_attention(q, k, v):
    b, s = q.shape[:2]
    nb = s // Q_BLOCK
    qb = q.reshape(b, nb, Q_BLOCK, N_KV, KV_GROUP, HEAD_DIM).transpose(1, 0, 2, 3, 4, 5)
    scale = HEAD_DIM ** -0.5

    def one_block(qblk):
        sc = jnp.einsum("bqkgd,btkd->bkgqt", qblk, k).astype(jnp.float32) * scale
        p = jax.nn.softmax(sc, axis=-1).astype(v.dtype)
        return jnp.einsum("bkgqt,btkd->bqkgd", p, v)

    out = lax.map(one_block, qb)
    return out.transpose(1, 0, 2, 3, 4, 5).reshape(b, s, ATTN_DIM)


def _short_conv(u, w):
    up = jnp.pad(u, ((0, 0), (1, 1), (0, 0)))
    return w[0] * up[:, :-2] + w[1] * up[:, 1:-1] + w[2] * up[:, 2:]


def _mixer(h, rope, ctx_k, ctx_v, w_in, q_norm, k_norm, conv_w, attn_out_norm, conv_out_norm, w_out):
    b, s, _ = h.shape
    proj = h @ w_in
    q, k, v, gb, gc, u = jnp.split(
        proj, np.cumsum([ATTN_DIM, KV_DIM, KV_DIM, CONV_DIM, CONV_DIM]).tolist(), axis=-1)
    q = _rms(q.reshape(b, s, N_HEADS, HEAD_DIM), q_norm)
    k = _rms(k.reshape(b, s, N_KV, HEAD_DIM), k_norm)
    v = v.reshape(b, s, N_KV, HEAD_DIM)
    if rope is None:
        attn = _block_attention(q, k, v)
    else:
        cos, sin = rope
        q_r = _apply_rope(q, cos, sin)
        k_r = _apply_rope(k, cos, sin)
        k_all = jnp.concatenate([ctx_k, k_r], axis=1)
        v_all = jnp.concatenate([ctx_v, v], axis=1)
        attn = _block_attention(q_r, k_all, v_all)
    y_conv = gb * _short_conv(gc * u, conv_w)
    merged = jnp.concatenate([_rms(attn, attn_out_norm), _rms(y_conv, conv_out_norm)], axis=-1)
    return merged @ w_out, k, v


def _layer(x, cond, rope, ctx_k, ctx_v, norm_mix, norm_ffn, w_ada, b_ada, w_in, q_norm, k_norm,
           conv_w, attn_out_norm, conv_out_norm, w_out, w_gate_up, w_down):
    mod = (jax.nn.silu(cond) @ w_ada + b_ada)[:, None, :]
    sh1, sc1, g1, sh2, sc2, g2 = jnp.split(mod, 6, axis=-1)
    h = _rms(x, norm_mix) * (1 + sc1) + sh1
    mix, k, v = _mixer(h, rope, ctx_k, ctx_v, w_in, q_norm, k_norm, conv_w,
                       attn_out_norm, conv_out_norm, w_out)
    x = x + g1 * mix
    h2 = _rms(x, norm_ffn) * (1 + sc2) + sh2
    gate, up = jnp.split(h2 @ w_gate_up, 2, axis=-1)
    x = x + g2 * ((jax.nn.silu(gate) * up) @ w_down)
    return x, k, v


def setup_inputs(seed: int = 0) -> dict:
    key = jax.random.key(seed)
    ks = jax.random.split(key, 20)
    f = jnp.float32
    nrm = lambda k, shape, s: jax.random.normal(k, shape, f) * s
    return {
        "x_prompt": nrm(ks[0], (BATCH, SEQ, D_MODEL), 1.0),
        "x_sample": nrm(ks[1], (DEC_BATCH, DEC_SEQ, D_MODEL), 1.0),
        "c": nrm(ks[2], (DEC_BATCH, D_MODEL), 1.0),
        "cache_k": nrm(ks[3], (DEC_BATCH, DEPTH, PAST_LEN, N_KV, HEAD_DIM), 1.0),
        "cache_v": nrm(ks[4], (DEC_BATCH, DEPTH, PAST_LEN, N_KV, HEAD_DIM), 1.0),
        "c_ctx": nrm(ks[5], (D_MODEL,), 1.0),
        "norm_mix": 1.0 + nrm(ks[6], (DEPTH, D_MODEL), 0.02),
        "norm_ffn": 1.0 + nrm(ks[7], (DEPTH, D_MODEL), 0.02),
        "w_ada": nrm(ks[8], (DEPTH, D_MODEL, 6 * D_MODEL), 0.5 * D_MODEL ** -0.5),
        "b_ada": nrm(ks[9], (DEPTH, 6 * D_MODEL), 0.02),
        "w_in": nrm(ks[10], (DEPTH, D_MODEL, IN_DIM), D_MODEL ** -0.5),
        "q_norm": 1.0 + nrm(ks[11], (DEPTH, HEAD_DIM), 0.02),
        "k_norm": 1.0 + nrm(ks[12], (DEPTH, HEAD_DIM), 0.02),
        "conv_w": nrm(ks[13], (DEPTH, CONV_WIDTH, CONV_DIM), CONV_WIDTH ** -0.5),
        "attn_out_norm": 1.0 + nrm(ks[14], (DEPTH, ATTN_DIM), 0.02),
        "conv_out_norm": 1.0 + nrm(ks[15], (DEPTH, CONV_DIM), 0.02),
        "w_out": nrm(ks[16], (DEPTH, D_MODEL, D_MODEL), D_MODEL ** -0.5),
        "w_gate_up": nrm(ks[17], (DEPTH, D_MODEL, 2 * D_FF), D_MODEL ** -0.5),
        "w_down": nrm(ks[18], (DEPTH, D_FF, D_MODEL), D_FF ** -0.5),
    }


def reference(x_prompt, x_sample, c, cache_k, cache_v, c_ctx, norm_mix, norm_ffn, w_ada, b_ada,
              w_in, q_norm, k_norm, conv_w, attn_out_norm, conv_out_norm, w_out, w_gate_up, w_down):
    rope = _axial_rope_tables(x_sample.shape[1], x_sample.dtype)
    cond_ctx = c_ctx[None, :]
    xp = x_prompt
    xs = x_sample
    new_k, new_v = [], []
    for l in range(DEPTH):
        params = (norm_mix[l], norm_ffn[l], w_ada[l], b_ada[l], w_in[l], q_norm[l], k_norm[l],
                  conv_w[l], attn_out_norm[l], conv_out_norm[l], w_out[l], w_gate_up[l], w_down[l])
        xp, k_l, v_l = _layer(xp, cond_ctx, None, None, None, *params)
        new_k.append(k_l)
        new_v.append(v_l)
        xs, _, _ = _layer(xs, c, rope, cache_k[:, l], cache_v[:, l], *params)
    ctx_k = jnp.stack(new_k, axis=1)
    ctx_v = jnp.stack(new_v, axis=1)
    return (xp, xs, ctx_k, ctx_v)
```

```python
import contextlib
import numpy as np
import concourse.bass as bass
import concourse.mybir as mybir
from concourse.bass_utils import run_bass_kernel_spmd

F32 = mybir.dt.float32
BF16 = mybir.dt.bfloat16
AF = mybir.ActivationFunctionType
ALU = mybir.AluOpType
AX = mybir.AxisListType

D = 1024
NT = 12
NTOK = NT * 128
DFF = 2816
NF = DFF // 128
IN_DIM = 2304
EPS = 1e-6
N_CORES = 8
CONST_BASE = 196 * 1024


class _Ins:
    __slots__ = ("eng", "fn", "deps", "dma", "sig", "cnt", "stream", "sval")

    def __init__(self, eng, fn, stream):
        self.eng = eng
        self.fn = fn
        self.deps = set()
        self.dma = stream is not None
        self.sig = False
        self.cnt = None
        self.stream = stream
        self.sval = None


class _Res:
    __slots__ = ("w", "r")

    def __init__(self):
        self.w = None
        self.r = []


def _esize(dt):
    return 2 if dt == BF16 else 4


def _regions(ap, excl_out):
    name = ap.tensor.name
    es = _esize(ap.dtype)
    dims = ap.ap
    if name.startswith("PS"):
        pstride = dims[0][0]
        off = ap.offset % pstride if pstride else ap.offset
        ext = 1
        for st, cn in dims[1:]:
            ext += st * (cn - 1)
        lo = off * es
        hi = (off + ext) * es
        excl_out.append(True)
        return [(name, b) for b in range(lo // 2048, (hi - 1) // 2048 + 1)]
    if name == "ARENA":
        pstride = dims[0][0]
        off = ap.offset % pstride if pstride else ap.offset
        ext = 1
        for st, cn in dims[1:]:
            ext += st * (cn - 1)
        lo = off * es
        hi = (off + ext) * es
        excl_out.append(False)
        if lo >= CONST_BASE:
            offs = [off]
            for st, cn in dims[1:]:
                if st == 0:
                    continue
                offs = [o + st * q for o in offs for q in range(cn)]
            return list({(name, "c", (o * es) // 4) for o in offs})
        return [(name, p) for p in range(lo // 256, (hi - 1) // 256 + 1)]
    if name.startswith("scr"):
        excl_out.append(False)
        return [(name, 0)]
    excl_out.append(False)
    return []


class Prog:
    ENGS = ("pe", "act", "dve", "pool", "sp")

    def __init__(self, nc):
        self.nc = nc
        self.ins = {e: [] for e in self.ENGS}
        self.res = {}
        self.streams = {}
        self.final_dmas = []

    def op(self, eng, fn, r=(), w=(), dma=None, final=False):
        i = _Ins(eng, fn, dma)
        if dma is not None:
            n = self.streams.get(dma, 0) + 1
            self.streams[dma] = n
            i.sval = 16 * n
        deps = set()
        raw = set()
        rkeys = []
        wkeys = []
        for a in r:
            ex = []
            ks = _regions(a, ex)
            (wkeys if ex[0] else rkeys).extend(ks)
        for a in w:
            ex = []
            wkeys.extend(_regions(a, ex))
        for k in rkeys:
            st = self.res.get(k)
            if st is None:
                st = self.res[k] = _Res()
            if st.w is not None:
                deps.add(st.w)
                raw.add(st.w)
            if i.dma:
                st.r.append(i)
            else:
                st.r = [x for x in st.r if x.dma or x.eng != eng]
                st.r.append(i)
        for k in wkeys:
            st = self.res.get(k)
            if st is None:
                st = self.res[k] = _Res()
            if st.w is not None:
                deps.add(st.w)
                if k[0].startswith("PS"):
                    raw.add(st.w)
            for x in st.r:
                deps.add(x)
            st.w = i
            st.r = []
        deps.discard(i)
        keep = set()
        for d in deps:
            if d.eng == eng and not d.dma and not i.dma:
                if eng == "pe":
                    continue
            keep.add(d)
        i.deps = keep
        for d in keep:
            d.sig = True
        self.ins[eng].append(i)
        if final:
            self.final_dmas.append(i)
        return i

    def emit(self):
        nc = self.nc
        with contextlib.ExitStack() as es:
            esem = {e: es.enter_context(nc.semaphore("s_" + e)) for e in self.ENGS}
            ssem = {s: es.enter_context(nc.semaphore("d_" + s)) for s in self.streams}
            for e in self.ENGS:
                c = 0
                for i in self.ins[e]:
                    if not i.dma and i.sig:
                        c += 1
                        i.cnt = c
            block = es.enter_context(nc.Block())

            def run(e, eng):
                waited = {}
                for i in self.ins[e]:
                    need = {}
                    for d in i.deps:
                        if d.dma:
                            key = ("d", d.stream)
                            v = d.sval
                        else:
                            key = ("e", d.eng)
                            v = d.cnt
                        if need.get(key, 0) < v:
                            need[key] = v
                    for key, v in need.items():
                        if waited.get(key, 0) >= v:
                            continue
                        waited[key] = v
                        eng.wait_ge(ssem[key[1]] if key[0] == "d" else esem[key[1]], v)
                    rr = i.fn(eng)
                    if i.dma:
                        rr.then_inc(ssem[i.stream], 16)
                    elif i.sig:
                        rr.then_inc(esem[e], 1)
                if e == "sp":
                    need = {}
                    for d in self.final_dmas:
                        if need.get(d.stream, 0) < d.sval:
                            need[d.stream] = d.sval
                    for s, v in need.items():
                        eng.wait_ge(ssem[s], v)

            @block.tensor
            def _(eng):
                run("pe", eng)

            @block.scalar
            def _(eng):
                run("act", eng)

            @block.vector
            def _(eng):
                run("dve", eng)

            @block.gpsimd
            def _(eng):
                run("pool", eng)

            @block.sync
            def _(eng):
                run("sp", eng)


def cap(ap, dims, off=0):
    return bass.AP(ap.tensor, ap.offset + off, [list(ap.ap[0])] + [list(d) for d in dims])


def build_program():
    nc = bass.Bass("TRN2", target_bir_lowering=False)

    def din(name, shape):
        return nc.dram_tensor(name, list(shape), F32, kind="ExternalInput").ap()

    x_d = din("x", [NTOK, D])
    condT_d = din("condT", [128, 8, 2])
    ck_d = din("cache_k", [256, 128])
    cv_d = din("cache_v", [256, 128])
    nmix_d = din("norm_mix", [1, D])
    nffn_d = din("norm_ffn", [1, D])
    wada_d = din("w_ada", [D, 6 * D])
    bada_d = din("b_ada", [1, 6 * D])
    win_d = din("w_in", [D, IN_DIM])
    qn_d = din("q_norm", [1, 64])
    kn_d = din("k_norm", [1, 64])
    convw_d = din("convw_fm", [128, 4, 3])
    aon_d = din("attn_out_norm", [1, 512])
    con_d = din("con_fm", [128, 4])
    wout_d = din("w_out", [D, D])
    wgu_d = din("w_gate_up", [D, 2 * DFF])
    wdn_d = din("w_down", [DFF, D])
    ident_d = din("ident", [128, 128])
    cos_d = din("rope_cos", [1024, 32])
    sin_d = din("rope_sin", [1024, 32])
    y_d = nc.dram_tensor("y", [NTOK, D], F32, kind="ExternalOutput").ap()
    newk_d = nc.dram_tensor("newk", [512, 128], F32, kind="ExternalOutput").ap()
    newv_d = nc.dram_tensor("newv", [512, 128], F32, kind="ExternalOutput").ap()
    scr_d = nc.dram_tensor("scr_mod", [2, 6 * D], F32, kind="Internal").ap()

    ARENA_BYTES = 200 * 1024
    ARENA = nc.alloc_sbuf_tensor("ARENA", [128, ARENA_BYTES // 2], BF16)
    PSF = nc.alloc_psum_tensor("PSF", [128, 8, 512], F32)

    def av(off, dt, shape, parts=128):
        es = _esize(dt)
        n = 1
        for s in shape:
            n *= s
        assert off % 4 == 0 and off + n * es <= ARENA_BYTES, (off, n, es)
        a = ARENA[0:parts, off // 2: off // 2 + n * es // 2]
        if dt != BF16:
            a = a.bitcast(dt)
        if len(shape) == 2:
            a = a.rearrange("p (a b) -> p a b", a=shape[0])
        elif len(shape) == 3:
            a = a.rearrange("p (a b c) -> p a b c", a=shape[0], b=shape[1])
        elif len(shape) == 4:
            a = a.rearrange("p (a b c d) -> p a b c d", a=shape[0], b=shape[1], c=shape[2])
        return a

    K = 1024
    R_XRES, R_WIN, R_HT, R_MT, R_WOUT, R_RING, R_TEMP = (
        0, 48 * K, 84 * K, 108 * K, 132 * K, 148 * K, 164 * K)
    R_CONST = 196 * K

    P = Prog(nc)
    op = P.op

    co = [R_CONST]

    def calloc(dt, shape, parts=128):
        es = _esize(dt)
        n = 1
        for s in shape:
            n *= s
        o = co[0]
        co[0] += ((n * es + 31) // 32) * 32
        assert co[0] <= 200 * K
        return av(o, dt, shape, parts)

    ident_bf = calloc(BF16, [128])
    ones_bf = calloc(BF16, [128])
    eps_c = calloc(F32, [1])
    sc_bf = calloc(BF16, [8, 2])
    convw = calloc(F32, [4, 3])
    con = calloc(F32, [4])
    gq_b = calloc(F32, [64])
    gk_b = calloc(F32, [64])
    ss_c = calloc(F32, [16])
    sd_c = calloc(F32, [16])
    rs_c = calloc(F32, [16])
    ss8 = calloc(F32, [2, 8])
    sd8 = calloc(F32, [2, 8])
    r8 = calloc(F32, [2, 8])
    rden = calloc(F32, [2, 4])
    cos_t = calloc(F32, [8, 32])
    sin_t = calloc(F32, [8, 32])

    rotF = [0]

    nrot = [4]

    def fbank():
        b = rotF[0] % nrot[0]
        rotF[0] += 1
        return b

    rotB = [0]

    def bbank():
        b = 6 + rotB[0] % 2
        rotB[0] += 1
        return b

    def skew(n_items, stages, defer=None):
        ns = len(stages)
        out = []
        for step in range(n_items + ns - 1):
            for s_ in range(ns):
                it = step - s_
                if 0 <= it < n_items:
                    if defer and (s_, it) in defer:
                        out.append((lambda s_=s_, it=it: stages[s_](it)))
                    else:
                        stages[s_](it)
        return out

    op("pool", lambda e: e.dma_start(out=ident_bf, in_=ident_d), w=[ident_bf], dma="ident")
    op("dve", lambda e: e.memset(ones_bf, 1.0), w=[ones_bf])
    op("dve", lambda e: e.memset(eps_c, EPS), w=[eps_c])
    condT = av(R_HT, F32, [8, 2])
    op("sp", lambda e: e.dma_start(out=condT, in_=condT_d), w=[condT], dma="condT")
    op("sp", lambda e: e.dma_start(out=convw, in_=convw_d), w=[convw], dma="convw")
    op("sp", lambda e: e.dma_start(out=con, in_=con_d), w=[con], dma="con")
    op("sp", lambda e: e.dma_start(out=gq_b, in_=qn_d.broadcast_to([128, 64])), w=[gq_b], dma="gq")
    op("sp", lambda e: e.dma_start(out=gk_b, in_=kn_d.broadcast_to([128, 64])), w=[gk_b], dma="gk")
    op("sp", lambda e: e.dma_start(out=cos_t, in_=cos_d.rearrange("(t p) d -> p t d", p=128)),
       w=[cos_t], dma="cos")
    op("sp", lambda e: e.dma_start(out=sin_t, in_=sin_d.rearrange("(t p) d -> p t d", p=128)),
       w=[sin_t], dma="sin")
    op("act", lambda e: e.activation(out=sc_bf, in_=condT, func=AF.Silu), r=[condT], w=[sc_bf])

    bada_sb = av(R_MT, F32, [6 * D], parts=2)
    nm_sb = av(R_WOUT, F32, [2 * D], parts=2)
    mstage = [av(R_WOUT + 8 * K + s_ * K, F32, [256], parts=2) for s_ in range(2)]
    op("sp", lambda e: e.dma_start(out=bada_sb, in_=bada_d.broadcast_to([2, 6 * D])), w=[bada_sb], dma="bada")
    op("sp", lambda e: e.dma_start(out=nm_sb[:, 0:D], in_=nmix_d.broadcast_to([2, D])),
       w=[nm_sb[:, 0:D]], dma="nmix")
    op("sp", lambda e: e.dma_start(out=nm_sb[:, D:2 * D], in_=nffn_d.broadcast_to([2, D])),
       w=[nm_sb[:, D:2 * D]], dma="nffn")

    NBLK = 24
    NRA = 4
    ring_ada = [av(R_RING + s_ * 4 * K, BF16, [8, 256]) for s_ in range(NRA)]
    wada_v = wada_d.rearrange("(k p) n -> p k n", p=128)

    def load_ada(blk):
        s_ = blk % NRA
        dst = ring_ada[s_]
        src = wada_v[:, :, blk * 256:(blk + 1) * 256]
        op("pool", lambda e: e.dma_start(out=dst, in_=src), w=[dst], dma="ada%d" % s_)

    WQKV = av(R_WIN, BF16, [8, 768])
    WCONV = av(R_WIN + 12 * K, BF16, [8, 1536])
    win_v = win_d.rearrange("(k p) n -> p k n", p=128)

    def load_wqkv():
        for kk in range(2):
            dst = WQKV[:, 4 * kk:4 * kk + 4, :]
            src = win_v[:, 4 * kk:4 * kk + 4, 0:768]
            op("pool", lambda e, dst=dst, src=src: e.dma_start(out=dst, in_=src), w=[dst], dma="wqkv%d" % kk)

    def load_wconv(gate):
        for kk in range(4):
            dst = WCONV[:, 2 * kk:2 * kk + 2, :]
            src = win_v[:, 2 * kk:2 * kk + 2, 768:IN_DIM]
            op("pool", lambda e, dst=dst, src=src: e.dma_start(out=dst, in_=src), r=[gate], w=[dst],
               dma="wconv%d" % kk)

    for blk_ in range(NRA):
        load_ada(blk_)

    xring = [av(R_XRES + s_ * 4 * K, F32, [D]) for s_ in range(3)]

    def load_x0(t):
        dst = xring[t % 3]
        op("sp", lambda e: e.dma_start(out=dst, in_=x_d[t * 128:(t + 1) * 128, :]), w=[dst], dma="x0_%d" % (t % 3))

    for t in range(3):
        load_x0(t)

    def mod_block(blk):
        b = 4 + blk % 2
        ps = PSF[0:2, b, 0:256]
        slot = ring_ada[blk % NRA]
        for k in range(8):
            op("pe", lambda e, ps=ps, k=k, slot=slot: e.matmul(ps, lhsT=sc_bf[:, k, :], rhs=slot[:, k, :],
                                                              start=(k == 0), stop=(k == 7)),
               r=[sc_bf, slot[:, k, :]], w=[ps])
        cols = slice(blk * 256, (blk + 1) * 256)
        ms = mstage[blk % 2]
        bsl = bada_sb[:, cols]
        op("dve", lambda e: e.tensor_tensor(out=ms, in0=ps, in1=bsl, op=ALU.add), r=[ps, bsl], w=[ms])
        sec = blk // 4
        if sec in (1, 4):
            ncol = 0 if sec == 1 else 1
            nsl = nm_sb[:, ncol * D + (blk % 4) * 256: ncol * D + (blk % 4 + 1) * 256]
            op("dve", lambda e: e.scalar_tensor_tensor(out=ms, in0=ms, scalar=1.0, in1=nsl,
                                                       op0=ALU.add, op1=ALU.mult), r=[ms, nsl], w=[ms])
        op("sp", lambda e: e.dma_start(out=scr_d[:, cols], in_=ms), r=[ms], w=[scr_d], dma="scrw%d" % (blk % 2))
        if blk + NRA < NBLK and not (4 <= blk < 8):
            load_ada(blk + NRA)

    for blk in range(8):
        mod_block(blk)
        if blk == 3:
            load_wqkv()

    def load_modb(dst, cond, col, stream):
        src = scr_d[cond:cond + 1, col * D:(col + 1) * D].broadcast_to([128, D])
        op("sp", lambda e: e.dma_start(out=dst, in_=src), r=[scr_d], w=[dst], dma=stream)

    gam1_b = [av(R_XRES + 12 * K + c * 8 * K, F32, [D]) for c in range(2)]
    sh1_b = [av(R_XRES + 16 * K + c * 8 * K, F32, [D]) for c in range(2)]
    for c in range(2):
        load_modb(gam1_b[c], c, 1, "g1b%d" % c)
        load_modb(sh1_b[c], c, 0, "s1b%d" % c)

    HT = av(R_HT, BF16, [8, NTOK])
    tmp0 = [av(R_XRES + 28 * K + s_ * 4 * K, F32, [D]) for s_ in range(4)]
    hb0 = [av(R_XRES + 44 * K + s_ * 2 * K, BF16, [D]) for s_ in range(2)]

    def norm_stats(xsrc, tmp, sscol):
        ssa = ss_c[:, sscol:sscol + 1]
        sda = sd_c[:, sscol:sscol + 1]
        op("act", lambda e: e.activation(out=tmp, in_=xsrc, func=AF.Square, accum_out=ssa),
           r=[xsrc], w=[tmp, ssa])
        op("act", lambda e: e.activation(out=sda, in_=ssa, func=AF.Sqrt, bias=eps_c[:, 0:1], scale=1.0 / D),
           r=[ssa, eps_c], w=[sda])

    def norm_apply(xsrc, gam_b, sh_b, tmp, hb, sscol, add_eng="pool"):
        sda = sd_c[:, sscol:sscol + 1]
        rsa = rs_c[:, sscol:sscol + 1]
        op("dve", lambda e: e.reciprocal(out=rsa, in_=sda), r=[sda], w=[rsa])
        op("dve", lambda e: e.scalar_tensor_tensor(out=tmp, in0=xsrc, scalar=rsa, in1=gam_b,
                                                   op0=ALU.mult, op1=ALU.mult),
           r=[xsrc, rsa, gam_b], w=[tmp])
        op(add_eng, lambda e: e.tensor_tensor(out=hb, in0=tmp, in1=sh_b, op=ALU.add), r=[tmp, sh_b], w=[hb])

    def transpose_to(hb, nchunk, dst3):
        bb = bbank()
        psb = PSF[:, bb, :].bitcast(BF16)
        for k in range(nchunk):
            pso = psb[:, k * 128:(k + 1) * 128]
            src = hb[:, k * 128:(k + 1) * 128]
            op("pe", lambda e, pso=pso, src=src: e.transpose(out=pso, in_=src, identity=ident_bf),
               r=[src, ident_bf], w=[pso])
        srcp = psb[:, 0:nchunk * 128].rearrange("p (k c) -> p k c", k=nchunk)
        op("act", lambda e: e.activation(out=dst3, in_=srcp, func=AF.Copy), r=[srcp], w=[dst3])

    xring = [av(R_XRES + s_ * 4 * K, F32, [D]) for s_ in range(3)]

    def p0_s0(t):
        norm_stats(xring[t % 3], tmp0[t % 4], t)

    def p0_s1(t):
        c = 0 if t < 8 else 1
        norm_apply(xring[t % 3], gam1_b[c], sh1_b[c], tmp0[t % 4], hb0[t % 2], t,
                   add_eng="dve")
        if t + 3 < NT:
            load_x0(t + 3)

    def p0_s2(t):
        transpose_to(hb0[t % 2], 8, HT[:, :, t * 128:(t + 1) * 128])

    skew(NT, [p0_s0, p0_s1, p0_s2])
    gate_ap = HT[:, :, (NT - 1) * 128:NT * 128]
    load_wconv(gate_ap)
    for blk_ in range(8, 8 + NRA):
        s__ = blk_ % NRA
        dst_ = ring_ada[s__]
        src_ = wada_v[:, :, blk_ * 256:(blk_ + 1) * 256]
        op("pool", lambda e, dst_=dst_, src_=src_: e.dma_start(out=dst_, in_=src_), r=[gate_ap], w=[dst_],
           dma="ada%d" % s__)

    MT = av(R_MT, BF16, [8, NTOK])
    QT = av(R_XRES, BF16, [4, NTOK])
    KT = av(R_XRES + 12 * K, BF16, [2, 1792])
    VO = av(R_XRES + 19 * K, BF16, [14, 2, 65])
    PT = [av(R_XRES + 23 * K + s_ * 2 * K, BF16, [2, 512]) for s_ in range(4)]
    attn_tm = av(R_XRES + 31 * K, F32, [4, 512])
    T0 = R_TEMP
    sq_sb = [av(T0 + s_ * 2 * K, F32, [512]) for s_ in range(2)]
    qn_sb = av(T0 + 4 * K, F32, [512])
    qg_sb = av(T0 + 6 * K, F32, [512])
    rA = av(T0 + 8 * K, F32, [256])
    rB = av(T0 + 9 * K, F32, [256])
    qr_bf = [av(T0 + 10 * K + s_ * K, BF16, [512]) for s_ in range(2)]
    kn_sb = av(T0 + 12 * K, F32, [128])
    kg_sb = [av(T0 + 12 * K + 512 + s_ * 512, F32, [128]) for s_ in range(2)]
    kf_sb = av(T0 + 14 * K, F32, [128])
    kd_bf = [av(T0 + 14 * K + 512 + s_ * 512, BF16, [256]) for s_ in range(2)]
    vs_sb = [av(T0 + 16 * K + s_ * 512, F32, [128]) for s_ in range(2)]
    ck_sb = av(T0 + 17 * K, F32, [2, 128])
    cv_sb = av(T0 + 18 * K, F32, [2, 128])
    aon_b = av(T0 + 19 * K, F32, [512])
    an_bf = [av(T0 + 21 * K + s_ * K, BF16, [512]) for s_ in range(2)]
    rsd = av(T0 + 23 * K, F32, [512])
    rsr = av(T0 + 25 * K, F32, [512])

    op("sp", lambda e: e.dma_start(out=ck_sb, in_=ck_d.rearrange("(t p) d -> p t d", p=128)), w=[ck_sb], dma="ck")
    op("sp", lambda e: e.dma_start(out=cv_sb, in_=cv_d.rearrange("(t p) d -> p t d", p=128)), w=[cv_sb], dma="cv")
    vo_ones = cap(VO, [(130, 14), (65, 2), (1, 1)], off=64)
    op("dve", lambda e: e.memset(vo_ones, 1.0), w=[VO])

    def k_dup(src_f32, kd, eng="dve"):
        dsto = cap(kd, [(128, 2), (64, 2), (1, 64)])
        srci = cap(src_f32, [(64, 2), (0, 2), (1, 64)])
        op(eng, lambda e: e.tensor_copy(out=dsto, in_=srci), r=[src_f32], w=[kd])

    def v_to_VO(src, vt):
        dst = VO[:, vt, :, 0:64]
        srcv = src.rearrange("p (k c) -> p k c", k=2)
        op("dve", lambda e: e.tensor_copy(out=dst, in_=srcv), r=[src], w=[dst])

    for tt in range(2):
        k_dup(ck_sb[:, tt, :], kd_bf[tt % 2])
        transpose_to(kd_bf[tt % 2], 2, KT[:, :, tt * 128:(tt + 1) * 128])
        v_to_VO(cv_sb[:, tt, :], tt)

    ropeT = {}
    for qi, (nm_, gsrc) in enumerate((("q", gq_b), ("k", gk_b))):
        for ti, (tab, goff) in enumerate(((cos_t, 0), (sin_t, 16), (sin_t, 0), (cos_t, 16))):
            dstT = av(T0 + 19 * K + (qi * 4 + ti) * K, F32, [8, 32])
            ropeT[(nm_, ti)] = dstT
            tv = cap(tab, [(32, 8), (16, 2), (1, 16)])
            gv = cap(gsrc, [(0, 8), (32, 2), (1, 16)], off=goff)
            ov = cap(dstT, [(32, 8), (16, 2), (1, 16)])
            op("dve", lambda e, tv=tv, gv=gv, ov=ov: e.tensor_tensor(out=ov, in0=tv, in1=gv, op=ALU.mult),
               r=[tab, gsrc], w=[dstT])
    rAk = av(T0 + 32000, F32, [64])
    rBk = av(T0 + 32256, F32, [64])
    sq_all = av(T0, F32, [640])
    s10 = cap(ss8, [(1, 10)])
    d10 = cap(sd8, [(1, 10)])
    r10 = cap(r8, [(1, 10)])

    def r10_col(i):
        return cap(r8, [(1, 1)], off=i)

    def rope_ops(src, dst, nh, tile, rA_, rB_, which):
        n = nh * 32
        x1 = cap(src, [(64, nh), (32, 2), (1, 16)])
        x2 = cap(src, [(64, nh), (32, 2), (1, 16)], off=16)
        o1 = cap(dst, [(64, nh), (32, 2), (1, 16)])
        o2 = cap(dst, [(64, nh), (32, 2), (1, 16)], off=16)
        tb = [cap(ropeT[(which, ti)][:, tile, :], [(0, nh), (16, 2), (1, 16)]) for ti in range(4)]
        tt_ = [ropeT[(which, ti)] for ti in range(4)]
        a = cap(rA_, [(32, nh), (16, 2), (1, 16)])
        b = cap(rB_, [(32, nh), (16, 2), (1, 16)])
        ra = rA_[:, 0:n]
        rb = rB_[:, 0:n]
        return [
            lambda: op("dve", lambda e: e.tensor_tensor(out=a, in0=x1, in1=tb[0], op=ALU.mult), r=[src, tt_[0]], w=[ra]),
            lambda: op("dve", lambda e: e.tensor_tensor(out=b, in0=x2, in1=tb[1], op=ALU.mult), r=[src, tt_[1]], w=[rb]),
            lambda: op("dve", lambda e: e.tensor_tensor(out=o1, in0=a, in1=b, op=ALU.subtract), r=[ra, rb], w=[dst]),
            lambda: op("dve", lambda e: e.tensor_tensor(out=a, in0=x1, in1=tb[2], op=ALU.mult), r=[src, tt_[2]], w=[ra]),
            lambda: op("dve", lambda e: e.tensor_tensor(out=b, in0=x2, in1=tb[3], op=ALU.mult), r=[src, tt_[3]], w=[rb]),
            lambda: op("dve", lambda e: e.tensor_tensor(out=o2, in0=a, in1=b, op=ALU.add), r=[ra, rb], w=[dst]),
        ]

    tm_banks = {}
    late_blk = [8]
    KENG = "dve"

    def p1_s0(t):
        tok = slice(t * 128, (t + 1) * 128)
        bq = fbank()
        bkv = fbank()
        tm_banks[t] = (bq, bkv)
        qps = PSF[:, bq, :]
        kvps = PSF[:, bkv, 0:256]
        for k in range(8):
            op("pe", lambda e, k=k: e.matmul(qps, lhsT=HT[:, k, tok], rhs=WQKV[:, k, 0:512],
                                             start=(k == 0), stop=(k == 7)),
               r=[HT[:, k, tok], WQKV[:, k, 0:512]], w=[qps])
        for k in range(8):
            op("pe", lambda e, k=k: e.matmul(kvps, lhsT=HT[:, k, tok], rhs=WQKV[:, k, 512:768],
                                             start=(k == 0), stop=(k == 7)),
               r=[HT[:, k, tok], WQKV[:, k, 512:768]], w=[kvps])
        if t >= 4:
            for _ in range(2):
                if late_blk[0] < NBLK:
                    mod_block(late_blk[0])
                    late_blk[0] += 1

    def p1_s1(t):
        sample = t < 8
        bq, bkv = tm_banks[t]
        qps = PSF[:, bq, :]
        kps = PSF[:, bkv, 0:128]
        vps = PSF[:, bkv, 128:256]
        op("act", lambda e: e.activation(out=sq_all[:, 0:512], in_=qps, func=AF.Square), r=[qps], w=[sq_all[:, 0:512]])
        op("act", lambda e: e.activation(out=sq_all[:, 512:640], in_=kps, func=AF.Square),
           r=[kps], w=[sq_all[:, 512:640]])
        op("dve", lambda e: e.tensor_reduce(out=s10, in_=sq_all.rearrange("p (h d) -> p h d", h=10),
                                            axis=AX.X, op=ALU.add), r=[sq_all], w=[s10])
        op("act", lambda e: e.activation(out=d10, in_=s10, func=AF.Sqrt, bias=eps_c[:, 0:1], scale=1.0 / 64),
           r=[s10, eps_c], w=[d10])
        op("dve", lambda e: e.reciprocal(out=r10, in_=d10), r=[d10], w=[r10])
        rb8 = cap(r10, [(1, 8), (0, 64)])
        rb2 = cap(r10, [(1, 2), (0, 64)], off=8)
        gqv = cap(gq_b, [(0, 8), (1, 64)])
        gkv = cap(gk_b, [(0, 2), (1, 64)])
        qr = qr_bf[t % 2]
        kg = kg_sb[t % 2]
        vs = vs_sb[t % 2]
        v3 = lambda x, h: x.rearrange("p (h d) -> p h d", h=h)
        QACT = 4
        qch = []
        for hq in range(QACT):
            qo = qn_sb[:, hq * 64:(hq + 1) * 64]
            qi = qps[:, hq * 64:(hq + 1) * 64]
            rq_ = r10_col(hq)
            qch.append(lambda qo=qo, qi=qi, rq_=rq_: op("act", lambda e: e.activation(out=qo, in_=qi, func=AF.Copy,
                                                                                       scale=rq_),
                                                        r=[qi, rq_], w=[qo]))
        nrest = 8 - QACT
        qo_r = qn_sb[:, QACT * 64:512].rearrange("p (h d) -> p h d", h=nrest)
        qi_r = qps[:, QACT * 64:512].rearrange("p (h d) -> p h d", h=nrest)
        rb_r = cap(r10, [(1, nrest), (0, 64)], off=QACT)
        qch.insert(0, lambda: op("dve", lambda e: e.tensor_tensor(out=qo_r, in0=qi_r, in1=rb_r, op=ALU.mult),
                                 r=[qps, r10], w=[qn_sb[:, QACT * 64:512]]))
        kch = []
        for hk in range(2):
            ko = kn_sb[:, hk * 64:(hk + 1) * 64]
            ki = kps[:, hk * 64:(hk + 1) * 64]
            rk = r10_col(8 + hk)
            kch.append(lambda ko=ko, ki=ki, rk=rk: op("act", lambda e: e.activation(out=ko, in_=ki, func=AF.Copy, scale=rk),
                                                      r=[ki, rk], w=[ko]))
        if sample:
            vdst = VO[:, 2 + t, :, 0:64]
            kch.append(lambda: op("act", lambda e: e.activation(out=vdst, in_=vps.rearrange("p (k c) -> p k c", k=2),
                                                                func=AF.Copy), r=[vps], w=[vdst]))
            qch += rope_ops(qn_sb, qr, 8, t, rA, rB, "q")
            kch += rope_ops(kn_sb, kf_sb, 2, t, rAk, rBk, "k")
            kch.append(lambda: k_dup(kf_sb, kd_bf[t % 2]))
        else:
            pt = t - 8
            vdst = VO[:, 10 + pt, :, 0:64]
            qch.append(lambda: op("dve", lambda e: e.tensor_tensor(out=v3(qr, 8), in0=v3(qn_sb, 8), in1=gqv,
                                                                 op=ALU.mult), r=[qn_sb, gq_b], w=[qr]))
            kch.append(lambda: op("dve", lambda e: e.tensor_tensor(out=v3(kg, 2), in0=v3(kn_sb, 2), in1=gkv,
                                                                 op=ALU.mult), r=[kn_sb, gk_b], w=[kg]))
            kch.append(lambda: op("sp", lambda e: e.dma_start(out=newk_d[pt * 128:(pt + 1) * 128, :], in_=kg),
                                  r=[kg], dma="nk%d" % (t % 2), final=True))
            kch.append(lambda: op("act", lambda e: e.activation(out=vs, in_=vps, func=AF.Copy), r=[vps], w=[vs]))
            kch.append(lambda: op("sp", lambda e: e.dma_start(out=newv_d[pt * 128:(pt + 1) * 128, :], in_=vs),
                                  r=[vs], dma="nv%d" % (t % 2), final=True))
            kch.append(lambda: k_dup(kg, kd_bf[t % 2]))
            kch.append(lambda: op("act", lambda e: e.activation(out=vdst, in_=vps.rearrange("p (k c) -> p k c", k=2),
                                                                func=AF.Copy), r=[vps], w=[vdst]))
        for ii in range(max(len(qch), len(kch))):
            if ii < len(qch):
                qch[ii]()
            if ii < len(kch):
                kch[ii]()

    def p1_s2(t):
        tok = slice(t * 128, (t + 1) * 128)
        transpose_to(qr_bf[t % 2], 4, QT[:, :, tok])
        keypos = 256 + t * 128 if t < 8 else 1280 + (t - 8) * 128
        transpose_to(kd_bf[t % 2], 2, KT[:, :, keypos:keypos + 128])

    skew(NT, [p1_s0, p1_s1, p1_s2])
    op("sp", lambda e: e.dma_start(out=aon_b, in_=aon_d.broadcast_to([128, 512])), w=[aon_b], dma="aon")
    while late_blk[0] < NBLK:
        mod_block(late_blk[0])
        late_blk[0] += 1

    CB = R_WOUT
    u_sb = [av(CB, F32, [1024]), av(R_MT, F32, [1024])]
    gb_sb = [av(CB + 4 * K, F32, [1024]), av(R_MT + 4 * K, F32, [1024])]
    t_pad = [av(CB + 8 * K, F32, [1032]), av(T0 + 27 * K, F32, [1032])]
    yall = av(CB + 8 * K + 4128, F32, [4, 1024])
    ysq = [av(T0 + 14 * K, BF16, [1024]), av(T0 + 4 * K, BF16, [1024])]
    yall_g = [yall, av(T0 + 6 * K, F32, [4, 512])]
    assert 8 * K + 4128 + 16 * K + 2 * K <= 32 * K

    items = []
    for grp in range(2):
        for c in range(4):
            items.append((grp, c))

    def grp_def(grp):
        if grp == 0:
            return 0, 1, 1024, [0, 1]
        return 1024, 2, 256, [2]

    def fm_s0(i):
        grp, c = items[i]
        tok0, seqs, L, blocks = grp_def(grp)
        padw = L + 2
        par = c % 2
        if c < 2:
            pads = cap(t_pad[par], [(padw, seqs), (L + 1, 2)])
            op("dve", lambda e: e.memset(pads, 0.0), w=[t_pad[par]])
        for bi, blk in enumerate(blocks):
            btok = slice(blk * 512, (blk + 1) * 512)
            loc = slice(bi * 512, (bi + 1) * 512)
            pss = []
            for part in range(3):
                col = part * 512 + c * 128
                b = fbank()
                ps = PSF[:, b, :]
                pss.append(ps)
                for k in range(8):
                    op("pe", lambda e, ps=ps, k=k, col=col, btok=btok: e.matmul(
                        ps, lhsT=WCONV[:, k, col:col + 128], rhs=HT[:, k, btok], start=(k == 0), stop=(k == 7)),
                       r=[WCONV[:, k, col:col + 128], HT[:, k, btok]], w=[ps])
            gbp, gcp, up = pss
            ul = u_sb[par][:, loc]
            gl = gb_sb[par][:, loc]
            op("act", lambda e, up=up, ul=ul: e.activation(out=ul, in_=up, func=AF.Copy), r=[up], w=[ul])
            op("act", lambda e, gbp=gbp, gl=gl: e.activation(out=gl, in_=gbp, func=AF.Copy), r=[gbp], w=[gl])
            if seqs == 1:
                tdst = t_pad[par][:, 1 + bi * 512: 1 + (bi + 1) * 512]
                tin0 = gcp
                tin1 = ul
            else:
                tdst = cap(t_pad[par], [(padw, seqs), (1, L)], off=1)
                tin0 = gcp.rearrange("p (s l) -> p s l", s=seqs)
                tin1 = ul.rearrange("p (s l) -> p s l", s=seqs)
            op("dve", lambda e, tdst=tdst, tin0=tin0, tin1=tin1: e.tensor_tensor(
                out=tdst, in0=tin0, in1=tin1, op=ALU.mult), r=[gcp, ul], w=[t_pad[par]])

    def fm_s1(i):
        grp, c = items[i]
        tok0, seqs, L, blocks = grp_def(grp)
        ntok = seqs * L
        padw = L + 2
        par = c % 2
        accb = [6 + bi for bi in range(len(blocks))]
        tp_ = t_pad[par]
        yv = cap(u_sb[par], [(L, seqs), (1, L)])
        t0 = cap(tp_, [(padw, seqs), (1, L)], off=0)
        t1 = cap(tp_, [(padw, seqs), (1, L)], off=1)
        t2 = cap(tp_, [(padw, seqs), (1, L)], off=2)
        w0 = convw[:, c, 0:1]
        w1 = convw[:, c, 1:2]
        w2 = convw[:, c, 2:3]
        uall = u_sb[par][:, 0:ntok]
        gall = gb_sb[par][:, 0:ntok]
        op("dve", lambda e: e.tensor_scalar(out=yv, in0=t0, scalar1=w0, scalar2=None, op0=ALU.mult),
           r=[tp_, convw], w=[uall])
        op("dve", lambda e: e.scalar_tensor_tensor(out=yv, in0=t1, scalar=w1, in1=yv, op0=ALU.mult, op1=ALU.add),
           r=[tp_, convw, uall], w=[uall])
        op("dve", lambda e: e.scalar_tensor_tensor(out=yv, in0=t2, scalar=w2, in1=yv, op0=ALU.mult, op1=ALU.add),
           r=[tp_, convw, uall], w=[uall])
        ydst = yall_g[grp][:, c, 0:ntok]
        op("dve", lambda e: e.tensor_tensor(out=ydst, in0=uall, in1=gall, op=ALU.mult), r=[uall, gall], w=[ydst])

    def fm_s1b(i):
        grp, c = items[i]
        tok0, seqs, L, blocks = grp_def(grp)
        ntok = seqs * L
        ydst = yall_g[grp][:, c, 0:ntok]
        ysl = ysq[i % 2][:, 0:ntok]
        op("act", lambda e: e.activation(out=ysl, in_=ydst, func=AF.Square), r=[ydst], w=[ysl])

    def fm_s2(i):
        grp, c = items[i]
        tok0, seqs, L, blocks = grp_def(grp)
        accb = [6 + bi for bi in range(len(blocks))]
        for bi in range(len(blocks)):
            aps = PSF[:, accb[bi], :]
            rh = ysq[i % 2][:, bi * 512:(bi + 1) * 512]
            op("pe", lambda e, aps=aps, rh=rh: e.matmul(aps, lhsT=ones_bf, rhs=rh, start=(c == 0), stop=(c == 3)),
               r=[ones_bf, rh], w=[aps])
        if c == 3:
            for bi, blk in enumerate(blocks):
                aps = PSF[:, accb[bi], :]
                op("act", lambda e, aps=aps: e.activation(out=rsd, in_=aps, func=AF.Sqrt, bias=eps_c[:, 0:1],
                                                          scale=1.0 / 512), r=[aps, eps_c], w=[rsd])
                op("dve", lambda e: e.reciprocal(out=rsr, in_=rsd), r=[rsd], w=[rsr])
                for cc_ in range(4):
                    dst = MT[:, 4 + cc_, blk * 512:(blk + 1) * 512]
                    ysrc = yall_g[grp][:, cc_, bi * 512:(bi + 1) * 512]
                    cc = con[:, cc_:cc_ + 1]
                    op("dve", lambda e, dst=dst, ysrc=ysrc, cc=cc: e.scalar_tensor_tensor(
                        out=dst, in0=ysrc, scalar=cc, in1=rsr, op0=ALU.mult, op1=ALU.mult),
                       r=[ysrc, con, rsr], w=[dst])

    nrot[0] = 6
    nI = len(items)
    fm_tail = skew(nI, [fm_s0, fm_s1, fm_s1b, fm_s2],
                   defer={(2, nI - 2), (2, nI - 1), (3, nI - 2), (3, nI - 1)})

    g1_b = [av(R_WIN + c * 12 * K, F32, [D]) for c in range(2)]
    gam2_b = [av(R_WIN + 4 * K + c * 12 * K, F32, [D]) for c in range(2)]
    sh2_b = [av(R_WIN + 8 * K + c * 12 * K, F32, [D]) for c in range(2)]
    for c in range(2):
        load_modb(g1_b[c], c, 2, "m_g1%d" % c)
        load_modb(gam2_b[c], c, 4, "m_gam2%d" % c)
        load_modb(sh2_b[c], c, 3, "m_sh2%d" % c)
    WOUT = av(R_WOUT, BF16, [8, D])
    wout_v = wout_d.rearrange("(k p) n -> p k n", p=128)
    op("pool", lambda e: e.dma_start(out=WOUT, in_=wout_v), w=[WOUT], dma="wout")
    ring_gu = [av(R_RING + s_ * 4 * K, BF16, [2, 8, 128]) for s_ in range(4)]
    wgu_v = wgu_d.rearrange("(k p) n -> p k n", p=128)

    def load_gu(f):
        s_ = f % 4
        dst = ring_gu[s_]
        op("pool", lambda e: e.dma_start(out=dst[:, 0], in_=wgu_v[:, :, f * 128:(f + 1) * 128]),
           w=[dst[:, 0]], dma="gua%d" % s_)
        op("pool", lambda e: e.dma_start(out=dst[:, 1],
                                         in_=wgu_v[:, :, DFF + f * 128:DFF + (f + 1) * 128]),
           w=[dst[:, 1]], dma="gub%d" % s_)

    for f_ in range(4):
        load_gu(f_)

    ptc = [0]
    pending = [(False, (lambda: None), -1)] * 5 + [(True, th, -1) for th in fm_tail]
    qbi = [0]
    attn_bufs = [attn_tm, av(R_XRES + 39 * K, F32, [4, 512])]
    seqdefs = [
        (0, 1024, 0, 10, 0),
        (1024, 256, 1280, 2, 10),
        (1280, 256, 1536, 2, 12),
    ]
    for (q0, nq, kb, nkc, vt0) in seqdefs:
        Nq = min(nq, 512)
        ntile = Nq // 128
        for qb in range(nq // Nq):
            qtok0 = q0 + qb * Nq
            attn_cur = attn_bufs[qbi[0] % 2]
            while pending and pending[0][2] <= qbi[0] - 2:
                pending.pop(0)[1]()
            jc = [(j, c) for j in range(4) for c in range(nkc)]
            pts = {}

            def at_s0(i, jc=jc, pts=pts, qtok0=qtok0, Nq=Nq, kb=kb):
                j, c = jc[i]
                kv = j // 2
                pb = (i % 2) * 2
                for hh in range(2):
                    pr = slice(hh * 64, hh * 64 + 64)
                    sps = PSF[:, pb + hh, 0:Nq]
                    lk = KT[pr, kv, kb + c * 128: kb + (c + 1) * 128]
                    rq = QT[pr, j, qtok0:qtok0 + Nq]
                    op("pe", lambda e, sps=sps, lk=lk, rq=rq: e.matmul(sps, lhsT=lk, rhs=rq, start=True, stop=True),
                       r=[lk, rq], w=[sps])
                pt = PT[ptc[0] % 4][:, :, 0:Nq]
                ptc[0] += 1
                pts[i] = pt
                for hh in range(2):
                    sph = PSF[:, pb + hh, 0:Nq]
                    pth = pt[:, hh, :]
                    op("act", lambda e, sph=sph, pth=pth: e.activation(out=pth, in_=sph, func=AF.Exp, scale=0.125),
                       r=[sph], w=[pth])

            def at_s1(i, jc=jc, pts=pts, ntile=ntile, nkc=nkc, vt0=vt0, attn_cur=attn_cur):
                j, c = jc[i]
                kv = j // 2
                pt = pts[i]
                vo = VO[:, vt0 + c, kv, :]
                for hh in range(2):
                    accbank = 4 + (j % 2) * 2 + hh
                    acc = PSF[:, accbank, 0:4 * 65].rearrange("p (t d) -> p t d", t=4)
                    for tq in range(ntile):
                        at = acc[:, tq, :]
                        lp = pt[:, hh, tq * 128:(tq + 1) * 128]
                        op("pe", lambda e, at=at, lp=lp, tq=tq: e.matmul(
                            at, lhsT=lp, rhs=vo, start=(c == 0 and tq == 0), stop=(c == nkc - 1),
                            skip_group_check=True), r=[lp, vo], w=[at])
                if c == nkc - 1:
                    for hh in range(2):
                        h = 2 * j + hh
                        accbank = 4 + (j % 2) * 2 + hh
                        acc = PSF[:, accbank, 0:4 * 65].rearrange("p (t d) -> p t d", t=4)
                        rd = rden[:, hh, 0:ntile]
                        den = cap(acc, [(65, ntile)], off=64)
                        op("dve", lambda e, rd=rd, den=den: e.reciprocal(out=rd, in_=den), r=[acc], w=[rd])
                        dst = attn_cur[:, 0:ntile, h * 64:(h + 1) * 64]
                        num = acc[:, 0:ntile, 0:64]
                        rdb = cap(rd, [(1, ntile), (0, 64)])
                        op("dve", lambda e, dst=dst, num=num, rdb=rdb: e.tensor_tensor(
                            out=dst, in0=num, in1=rdb, op=ALU.mult), r=[acc, rd], w=[dst])

            ns_ = 2
            for step in range(len(jc) + ns_ - 1):
                for s_ in range(ns_):
                    it = step - s_
                    if 0 <= it < len(jc):
                        (at_s0, at_s1)[s_](it)
                if pending:
                    jcur = jc[min(step, len(jc) - 1)][0]
                    if not (pending[0][0] and jcur % 2 == 1):
                        pending.pop(0)[1]()
            abuf = attn_cur
            cols = slice(12, 12 + ntile)

            def mk_sq(tq, abuf=abuf):
                src = abuf[:, tq, :]
                ssa = ss_c[:, 12 + tq: 13 + tq]
                sq = sq_sb[tq % 2]
                return lambda: op("act", lambda e: e.activation(out=sq, in_=src, func=AF.Square, accum_out=ssa),
                                  r=[src], w=[sq, ssa])

            def mk_rs(cols=cols):
                def f():
                    op("act", lambda e: e.activation(out=sd_c[:, cols], in_=ss_c[:, cols], func=AF.Ln,
                                                     bias=eps_c[:, 0:1], scale=1.0 / 512),
                       r=[ss_c[:, cols], eps_c], w=[sd_c[:, cols]])
                    op("act", lambda e: e.activation(out=rs_c[:, cols], in_=sd_c[:, cols], func=AF.Exp, scale=-0.5),
                       r=[sd_c[:, cols]], w=[rs_c[:, cols]])
                return f

            def mk_tr(tq, abuf=abuf, qtok0=qtok0):
                def f():
                    tglob = (qtok0 // 128) + tq
                    src = abuf[:, tq, :]
                    rsa = rs_c[:, 12 + tq: 13 + tq]
                    an = an_bf[tq % 2]
                    op("dve", lambda e: e.scalar_tensor_tensor(out=an, in0=src, scalar=rsa, in1=aon_b,
                                                               op0=ALU.mult, op1=ALU.mult),
                       r=[src, rsa, aon_b], w=[an])
                    transpose_to(an, 4, MT[:, 0:4, tglob * 128:(tglob + 1) * 128])
                return f

            for tq in range(ntile):
                pending.append((False, mk_sq(tq), qbi[0]))
            pending.append((False, mk_rs(), qbi[0]))
            for tq in range(ntile):
                pending.append((True, mk_tr(tq), qbi[0]))
            qbi[0] += 1
    while pending:
        pending.pop(0)[1]()

    XRES = av(R_XRES, F32, [NT, D])
    x3ring = [av(R_TEMP + s_ * 4 * K, F32, [D]) for s_ in range(3)]
    tmp3 = [av(R_TEMP + 12 * K + s_ * 4 * K, F32, [D]) for s_ in range(4)]
    hb3 = [av(R_TEMP + 28 * K + s_ * 2 * K, BF16, [D]) for s_ in range(2)]
    g1_b = [av(R_WIN + c * 12 * K, F32, [D]) for c in range(2)]
    gam2_b = [av(R_WIN + 4 * K + c * 12 * K, F32, [D]) for c in range(2)]
    sh2_b = [av(R_WIN + 8 * K + c * 12 * K, F32, [D]) for c in range(2)]
    g2_b = [av(R_WOUT + 8 * K + c * 4 * K, F32, [D]) for c in range(2)]

    def load_x3(t):
        dst = x3ring[t % 3]
        op("sp", lambda e: e.dma_start(out=dst, in_=x_d[t * 128:(t + 1) * 128, :]), w=[dst], dma="x3_%d" % (t % 3))

    for t in range(3):
        load_x3(t)
    p3_banks = {}

    def p3_s0(t):
        tok = slice(t * 128, (t + 1) * 128)
        bs = []
        for hh in range(2):
            b = fbank()
            bs.append(b)
            ps = PSF[:, b, :]
            for k in range(8):
                op("pe", lambda e, ps=ps, k=k, hh=hh: e.matmul(
                    ps, lhsT=MT[:, k, tok], rhs=WOUT[:, k, hh * 512:(hh + 1) * 512], start=(k == 0), stop=(k == 7)),
                   r=[MT[:, k, tok], WOUT[:, k, hh * 512:(hh + 1) * 512]], w=[ps])
        p3_banks[t] = bs

    def p3_s1(t):
        c = 0 if t < 8 else 1
        xs = x3ring[t % 3]
        for hh in range(2):
            ps = PSF[:, p3_banks[t][hh], :]
            cs = slice(hh * 512, (hh + 1) * 512)
            tm = tmp3[t % 4][:, cs]
            gsl = g1_b[c][:, cs]
            op("dve", lambda e, tm=tm, ps=ps, gsl=gsl: e.tensor_tensor(out=tm, in0=ps, in1=gsl, op=ALU.mult),
               r=[ps, gsl], w=[tm])
            xo = XRES[:, t, cs]
            xsl = xs[:, cs]
            op("dve", lambda e, xo=xo, tm=tm, xsl=xsl: e.tensor_tensor(out=xo, in0=tm, in1=xsl, op=ALU.add),
               r=[tm, xsl], w=[xo])
        if t + 3 < NT:
            load_x3(t + 3)
        norm_stats(XRES[:, t, :], tmp3[t % 4], t)

    def p3_s1b(t):
        c = 0 if t < 8 else 1
        norm_apply(XRES[:, t, :], gam2_b[c], sh2_b[c], tmp3[t % 4], hb3[t % 2], t,
                   add_eng="dve")

    def p3_s2(t):
        transpose_to(hb3[t % 2], 8, HT[:, :, t * 128:(t + 1) * 128])

    p3_tail = skew(NT, [p3_s0, p3_s1, p3_s1b, p3_s2], defer={(3, NT - 2), (3, NT - 1)})

    for c in range(2):
        load_modb(g2_b[c], c, 5, "m_g2%d" % c)
    ACTT = av(R_WIN, BF16, [11, NTOK])
    WDN = av(R_MT, BF16, [11, D])
    wdn_v = wdn_d.rearrange("(f p) n -> p f n", p=128)
    sg = [av(R_WOUT + s * 2 * K, F32, [512]) for s in range(2)]
    tm4 = [av(R_WOUT + 4 * K + s * 2 * K, F32, [512]) for s in range(2)]
    cnt = [0]
    for half in range(2):
        fl0 = half * 11
        wsrc = wdn_v[:, fl0:fl0 + 11, :]
        op("pool", lambda e, wsrc=wsrc: e.dma_start(out=WDN, in_=wsrc), w=[WDN], dma="wdn")
        for fl in range(11):
            f = fl0 + fl
            slot = ring_gu[f % 4]
            for tb in range(3):
                if tb == 2 and p3_tail:
                    for th in p3_tail:
                        th()
                    p3_tail = []
                btok = slice(tb * 512, (tb + 1) * 512)
                bg = fbank()
                bu = fbank()
                gps = PSF[:, bg, :]
                ups = PSF[:, bu, :]
                for (ps, c0) in ((gps, 0), (ups, 1)):
                    for k in range(8):
                        op("pe", lambda e, ps=ps, k=k, c0=c0, btok=btok, slot=slot: e.matmul(
                            ps, lhsT=slot[:, c0, k, :], rhs=HT[:, k, btok], start=(k == 0), stop=(k == 7)),
                           r=[slot, HT[:, k, btok]], w=[ps])
                s_ = sg[cnt[0] % 2]
                cnt[0] += 1
                op("act", lambda e, s_=s_, gps=gps: e.activation(out=s_, in_=gps, func=AF.Silu), r=[gps], w=[s_])
                dst = ACTT[:, fl, btok]
                op("dve", lambda e, dst=dst, s_=s_, ups=ups: e.tensor_tensor(out=dst, in0=s_, in1=ups, op=ALU.mult),
                   r=[s_, ups], w=[dst])
            if f + 4 < NF:
                load_gu(f + 4)
        for t in range(NT):
            c = 0 if t < 8 else 1
            tok = slice(t * 128, (t + 1) * 128)
            for hh in range(2):
                b = fbank()
                ps = PSF[:, b, :]
                cs = slice(hh * 512, (hh + 1) * 512)
                for fl in range(11):
                    op("pe", lambda e, ps=ps, fl=fl, tok=tok, cs=cs: e.matmul(
                        ps, lhsT=ACTT[:, fl, tok], rhs=WDN[:, fl, cs], start=(fl == 0), stop=(fl == 10)),
                       r=[ACTT[:, fl, tok], WDN[:, fl, cs]], w=[ps])
                tm = tm4[cnt[0] % 2]
                cnt[0] += 1
                op("dve", lambda e, tm=tm, ps=ps, c=c, cs=cs: e.tensor_tensor(out=tm, in0=ps, in1=g2_b[c][:, cs],
                                                                             op=ALU.mult),
                   r=[ps, g2_b[c][:, cs]], w=[tm])
                xo = XRES[:, t, cs]
                op("dve", lambda e, xo=xo, tm=tm: e.tensor_tensor(out=xo, in0=xo, in1=tm, op=ALU.add),
                   r=[xo, tm], w=[xo])
            if half == 1:
                xt = XRES[:, t, :]
                op("sp", lambda e, xt=xt, t=t: e.dma_start(out=y_d[t * 128:(t + 1) * 128, :], in_=xt),
                   r=[xt], dma="yout%d" % (t % 4), final=True)

    P.emit()
    return nc


def _rope_tables():
    rows = 1024 // 64
    row = np.repeat(np.arange(rows, dtype=np.float64), 64)
    col = np.tile(np.arange(64, dtype=np.float64), rows)
    inv = 1.0 / (10000.0 ** (np.arange(16, dtype=np.float64) / 16.0))
    ang = np.stack([row[:, None] * inv, col[:, None] * inv], axis=1)
    return (np.cos(ang).astype(np.float32).reshape(1024, 32),
            np.sin(ang).astype(np.float32).reshape(1024, 32))


_CACHE = {}


def kernel(x_prompt, x_sample, c, cache_k, cache_v, c_ctx, norm_mix, norm_ffn, w_ada, b_ada,
           w_in, q_norm, k_norm, conv_w, attn_out_norm, conv_out_norm, w_out, w_gate_up, w_down):
    f = lambda a: np.ascontiguousarray(np.asarray(a, dtype=np.float32))
    x_prompt, x_sample, c, cache_k, cache_v, c_ctx = map(f, (x_prompt, x_sample, c, cache_k, cache_v, c_ctx))
    if "nc" not in _CACHE:
        _CACHE["nc"] = build_program()
    nc = _CACHE["nc"]
    cos, sin = _rope_tables()
    shared = {
        "norm_mix": f(norm_mix).reshape(1, D),
        "norm_ffn": f(norm_ffn).reshape(1, D),
        "w_ada": f(w_ada).reshape(D, 6 * D),
        "b_ada": f(b_ada).reshape(1, 6 * D),
        "w_in": f(w_in).reshape(D, IN_DIM),
        "q_norm": f(q_norm).reshape(1, 64),
        "k_norm": f(k_norm).reshape(1, 64),
        "convw_fm": f(np.transpose(f(conv_w).reshape(3, 4, 128), (2, 1, 0))),
        "attn_out_norm": f(attn_out_norm).reshape(1, 512),
        "con_fm": f(f(conv_out_norm).reshape(4, 128).T),
        "w_out": f(w_out).reshape(D, D),
        "w_gate_up": f(w_gate_up).reshape(D, 2 * DFF),
        "w_down": f(w_down).reshape(DFF, D),
        "ident": np.eye(128, dtype=np.float32),
        "rope_cos": cos,
        "rope_sin": sin,
    }
    in_maps = []
    for i in range(N_CORES):
        xa = np.concatenate([x_sample[i], x_prompt[2 * i], x_prompt[2 * i + 1]], axis=0)
        cond = np.stack([c[i], c_ctx], axis=1)
        condT = f(np.transpose(cond.reshape(8, 128, 2), (1, 0, 2)))
        m = dict(shared)
        m["x"] = f(xa)
        m["condT"] = condT
        m["cache_k"] = f(cache_k[i, 0].reshape(256, 128))
        m["cache_v"] = f(cache_v[i, 0].reshape(256, 128))
        in_maps.append(m)
    res = run_bass_kernel_spmd(nc, in_maps, core_ids=list(range(N_CORES)))
    y_prompt = np.empty((16, 256, D), np.float32)
    y_sample = np.empty((8, 1024, D), np.float32)
    new_k = np.empty((16, 1, 256, 2, 64), np.float32)
    new_v = np.empty((16, 1, 256, 2, 64), np.float32)
    for i in range(N_CORES):
        r = res.results[i]
        y = np.asarray(r["y"])
        y_sample[i] = y[0:1024]
        y_prompt[2 * i] = y[1024:1280]
        y_prompt[2 * i + 1] = y[1280:1536]
        nk = np.asarray(r["newk"]).reshape(2, 256, 2, 64)
        nv = np.asarray(r["newv"]).reshape(2, 256, 2, 64)
        new_k[2 * i, 0] = nk[0]
        new_k[2 * i + 1, 0] = nk[1]
        new_v[2 * i, 0] = nv[0]
        new_v[2 * i + 1, 0] = nv[1]
    return (y_prompt, y_sample, new_k, new_v)
```

```python
import contextlib
import numpy as np
import concourse.bass as bass
import concourse.mybir as mybir
from concourse.bass_utils import run_bass_kernel_spmd

F32 = mybir.dt.float32
BF16 = mybir.dt.bfloat16
AF = mybir.ActivationFunctionType
ALU = mybir.AluOpType
AX = mybir.AxisListType

D = 1024
NT = 12
NTOK = NT * 128
DFF = 2816
NF = DFF // 128
IN_DIM = 2304
EPS = 1e-6
N_CORES = 8
CONST_BASE = 196 * 1024


class _Ins:
    __slots__ = ("eng", "fn", "deps", "dma", "sig", "cnt", "stream", "sval")

    def __init__(self, eng, fn, stream):
        self.eng = eng
        self.fn = fn
        self.deps = set()
        self.dma = stream is not None
        self.sig = False
        self.cnt = None
        self.stream = stream
        self.sval = None


class _Res:
    __slots__ = ("w", "r")

    def __init__(self):
        self.w = None
        self.r = []


def _esize(dt):
    return 2 if dt == BF16 else 4


def _regions(ap, excl_out):
    name = ap.tensor.name
    es = _esize(ap.dtype)
    dims = ap.ap
    if name.startswith("PS"):
        pstride = dims[0][0]
        off = ap.offset % pstride if pstride else ap.offset
        ext = 1
        for st, cn in dims[1:]:
            ext += st * (cn - 1)
        lo = off * es
        hi = (off + ext) * es
        excl_out.append(True)
        return [(name, b) for b in range(lo // 2048, (hi - 1) // 2048 + 1)]
    if name == "ARENA":
        pstride = dims[0][0]
        off = ap.offset % pstride if pstride else ap.offset
        ext = 1
        for st, cn in dims[1:]:
            ext += st * (cn - 1)
        lo = off * es
        hi = (off + ext) * es
        excl_out.append(False)
        if lo >= CONST_BASE:
            offs = [off]
            for st, cn in dims[1:]:
                if st == 0:
                    continue
                offs = [o + st * q for o in offs for q in range(cn)]
            return list({(name, "c", (o * es) // 4) for o in offs})
        return [(name, p) for p in range(lo // 256, (hi - 1) // 256 + 1)]
    if name.startswith("scr"):
        excl_out.append(False)
        return [(name, 0)]
    excl_out.append(False)
    return []


class Prog:
    ENGS = ("pe", "act", "dve", "pool", "sp")

    def __init__(self, nc):
        self.nc = nc
        self.ins = {e: [] for e in self.ENGS}
        self.res = {}
        self.streams = {}
        self.final_dmas = []

    def op(self, eng, fn, r=(), w=(), dma=None, final=False):
        i = _Ins(eng, fn, dma)
        if dma is not None:
            n = self.streams.get(dma, 0) + 1
            self.streams[dma] = n
            i.sval = 16 * n
        deps = set()
        raw = set()
        rkeys = []
        wkeys = []
        for a in r:
            ex = []
            ks = _regions(a, ex)
            (wkeys if ex[0] else rkeys).extend(ks)
        for a in w:
            ex = []
            wkeys.extend(_regions(a, ex))
        for k in rkeys:
            st = self.res.get(k)
            if st is None:
                st = self.res[k] = _Res()
            if st.w is not None:
                deps.add(st.w)
                raw.add(st.w)
            if i.dma:
                st.r.append(i)
            else:
                st.r = [x for x in st.r if x.dma or x.eng != eng]
                st.r.append(i)
        for k in wkeys:
            st = self.res.get(k)
            if st is None:
                st = self.res[k] = _Res()
            if st.w is not None:
                deps.add(st.w)
                if k[0].startswith("PS"):
                    raw.add(st.w)
            for x in st.r:
                deps.add(x)
            st.w = i
            st.r = []
        deps.discard(i)
        keep = set()
        for d in deps:
            if d.eng == eng and not d.dma and not i.dma:
                if eng == "pe":
                    continue
            keep.add(d)
        i.deps = keep
        for d in keep:
            d.sig = True
        self.ins[eng].append(i)
        if final:
            self.final_dmas.append(i)
        return i

    def emit(self):
        nc = self.nc
        with contextlib.ExitStack() as es:
            esem = {e: es.enter_context(nc.semaphore("s_" + e)) for e in self.ENGS}
            ssem = {s: es.enter_context(nc.semaphore("d_" + s)) for s in self.streams}
            for e in self.ENGS:
                c = 0
                for i in self.ins[e]:
                    if not i.dma and i.sig:
                        c += 1
                        i.cnt = c
            block = es.enter_context(nc.Block())

            def run(e, eng):
                waited = {}
                for i in self.ins[e]:
                    need = {}
                    for d in i.deps:
                        if d.dma:
                            key = ("d", d.stream)
                            v = d.sval
                        else:
                            key = ("e", d.eng)
                            v = d.cnt
                        if need.get(key, 0) < v:
                            need[key] = v
                    for key, v in need.items():
                        if waited.get(key, 0) >= v:
                            continue
                        waited[key] = v
                        eng.wait_ge(ssem[key[1]] if key[0] == "d" else esem[key[1]], v)
                    rr = i.fn(eng)
                    if i.dma:
                        rr.then_inc(ssem[i.stream], 16)
                    elif i.sig:
                        rr.then_inc(esem[e], 1)
                if e == "sp":
                    need = {}
                    for d in self.final_dmas:
                        if need.get(d.stream, 0) < d.sval:
                            need[d.stream] = d.sval
                    for s, v in need.items():
                        eng.wait_ge(ssem[s], v)

            @block.tensor
            def _(eng):
                run("pe", eng)

            @block.scalar
            def _(eng):
                run("act", eng)

            @block.vector
            def _(eng):
                run("dve", eng)

            @block.gpsimd
            def _(eng):
                run("pool", eng)

            @block.sync
            def _(eng):
                run("sp", eng)


def cap(ap, dims, off=0):
    return bass.AP(ap.tensor, ap.offset + off, [list(ap.ap[0])] + [list(d) for d in dims])


def build_program():
    nc = bass.Bass("TRN2", target_bir_lowering=False)

    def din(name, shape):
        return nc.dram_tensor(name, list(shape), F32, kind="ExternalInput").ap()

    x_d = din("x", [NTOK, D])
    condT_d = din("condT", [128, 8, 2])
    ck_d = din("cache_k", [256, 128])
    cv_d = din("cache_v", [256, 128])
    nmix_d = din("norm_mix", [1, D])
    nffn_d = din("norm_ffn", [1, D])
    wada_d = din("w_ada", [D, 6 * D])
    bada_d = din("b_ada", [1, 6 * D])
    win_d = din("w_in", [D, IN_DIM])
    qn_d = din("q_norm", [1, 64])
    kn_d = din("k_norm", [1, 64])
    convw_d = din("convw_fm", [128, 4, 3])
    aon_d = din("attn_out_norm", [1, 512])
    con_d = din("con_fm", [128, 4])
    wout_d = din("w_out", [D, D])
    wgu_d = din("w_gate_up", [D, 2 * DFF])
    wdn_d = din("w_down", [DFF, D])
    ident_d = din("ident", [128, 128])
    cos_d = din("rope_cos", [1024, 32])
    sin_d = din("rope_sin", [1024, 32])
    y_d = nc.dram_tensor("y", [NTOK, D], F32, kind="ExternalOutput").ap()
    newk_d = nc.dram_tensor("newk", [512, 128], F32, kind="ExternalOutput").ap()
    newv_d = nc.dram_tensor("newv", [512, 128], F32, kind="ExternalOutput").ap()
    scr_d = nc.dram_tensor("scr_mod", [2, 6 * D], F32, kind="Internal").ap()

    ARENA_BYTES = 200 * 1024
    ARENA = nc.alloc_sbuf_tensor("ARENA", [128, ARENA_BYTES // 2], BF16)
    PSF = nc.alloc_psum_tensor("PSF", [128, 8, 512], F32)

    def av(off, dt, shape, parts=128):
        es = _esize(dt)
        n = 1
        for s in shape:
            n *= s
        assert off % 4 == 0 and off + n * es <= ARENA_BYTES, (off, n, es)
        a = ARENA[0:parts, off // 2: off // 2 + n * es // 2]
        if dt != BF16:
            a = a.bitcast(dt)
        if len(shape) == 2:
            a = a.rearrange("p (a b) -> p a b", a=shape[0])
        elif len(shape) == 3:
            a = a.rearrange("p (a b c) -> p a b c", a=shape[0], b=shape[1])
        elif len(shape) == 4:
            a = a.rearrange("p (a b c d) -> p a b c d", a=shape[0], b=shape[1], c=shape[2])
        return a

    K = 1024
    R_XRES, R_WIN, R_HT, R_MT, R_WOUT, R_RING, R_TEMP = (
        0, 48 * K, 84 * K, 108 * K, 132 * K, 148 * K, 164 * K)
    R_CONST = 196 * K

    P = Prog(nc)
    op = P.op

    co = [R_CONST]

    def calloc(dt, shape, parts=128):
        es = _esize(dt)
        n = 1
        for s in shape:
            n *= s
        o = co[0]
        co[0] += ((n * es + 31) // 32) * 32
        assert co[0] <= 200 * K
        return av(o, dt, shape, parts)

    ident_bf = calloc(BF16, [128])
    ones_bf = calloc(BF16, [128])
    eps_c = calloc(F32, [1])
    sc_bf = calloc(BF16, [8, 2])
    convw = calloc(F32, [4, 3])
    con = calloc(F32, [4])
    gq_b = calloc(F32, [64])
    gk_b = calloc(F32, [64])
    ss_c = calloc(F32, [16])
    sd_c = calloc(F32, [16])
    rs_c = calloc(F32, [16])
    ss8 = calloc(F32, [2, 8])
    sd8 = calloc(F32, [2, 8])
    r8 = calloc(F32, [2, 8])
    rden = calloc(F32, [2, 4])
    cos_t = calloc(F32, [8, 32])
    sin_t = calloc(F32, [8, 32])

    rotF = [0]

    nrot = [4]

    def fbank():
        b = rotF[0] % nrot[0]
        rotF[0] += 1
        return b

    rotB = [0]

    def bbank():
        b = 6 + rotB[0] % 2
        rotB[0] += 1
        return b

    def skew(n_items, stages, defer=None):
        ns = len(stages)
        out = []
        for step in range(n_items + ns - 1):
            for s_ in range(ns):
                it = step - s_
                if 0 <= it < n_items:
                    if defer and (s_, it) in defer:
                        out.append((lambda s_=s_, it=it: stages[s_](it)))
                    else:
                        stages[s_](it)
        return out

    op("pool", lambda e: e.dma_start(out=ident_bf, in_=ident_d), w=[ident_bf], dma="ident")
    op("dve", lambda e: e.memset(ones_bf, 1.0), w=[ones_bf])
    op("dve", lambda e: e.memset(eps_c, EPS), w=[eps_c])
    condT = av(R_HT, F32, [8, 2])
    op("sp", lambda e: e.dma_start(out=condT, in_=condT_d), w=[condT], dma="condT")
    op("sp", lambda e: e.dma_start(out=convw, in_=convw_d), w=[convw], dma="convw")
    op("sp", lambda e: e.dma_start(out=con, in_=con_d), w=[con], dma="con")
    op("sp", lambda e: e.dma_start(out=gq_b, in_=qn_d.broadcast_to([128, 64])), w=[gq_b], dma="gq")
    op("sp", lambda e: e.dma_start(out=gk_b, in_=kn_d.broadcast_to([128, 64])), w=[gk_b], dma="gk")
    op("sp", lambda e: e.dma_start(out=cos_t, in_=cos_d.rearrange("(t p) d -> p t d", p=128)),
       w=[cos_t], dma="cos")
    op("sp", lambda e: e.dma_start(out=sin_t, in_=sin_d.rearrange("(t p) d -> p t d", p=128)),
       w=[sin_t], dma="sin")
    op("act", lambda e: e.activation(out=sc_bf, in_=condT, func=AF.Silu), r=[condT], w=[sc_bf])

    bada_sb = av(R_MT, F32, [6 * D], parts=2)
    nm_sb = av(R_WOUT, F32, [2 * D], parts=2)
    mstage = [av(R_WOUT + 8 * K + s_ * K, F32, [256], parts=2) for s_ in range(2)]
    op("sp", lambda e: e.dma_start(out=bada_sb, in_=bada_d.broadcast_to([2, 6 * D])), w=[bada_sb], dma="bada")
    op("sp", lambda e: e.dma_start(out=nm_sb[:, 0:D], in_=nmix_d.broadcast_to([2, D])),
       w=[nm_sb[:, 0:D]], dma="nmix")
    op("sp", lambda e: e.dma_start(out=nm_sb[:, D:2 * D], in_=nffn_d.broadcast_to([2, D])),
       w=[nm_sb[:, D:2 * D]], dma="nffn")

    NBLK = 24
    NRA = 4
    ring_ada = [av(R_RING + s_ * 4 * K, BF16, [8, 256]) for s_ in range(NRA)]
    wada_v = wada_d.rearrange("(k p) n -> p k n", p=128)

    def load_ada(blk):
        s_ = blk % NRA
        dst = ring_ada[s_]
        src = wada_v[:, :, blk * 256:(blk + 1) * 256]
        op("pool", lambda e: e.dma_start(out=dst, in_=src), w=[dst], dma="ada%d" % s_)

    WQKV = av(R_WIN, BF16, [8, 768])
    WCONV = av(R_WIN + 12 * K, BF16, [8, 1536])
    win_v = win_d.rearrange("(k p) n -> p k n", p=128)

    def load_wqkv():
        for kk in range(2):
            dst = WQKV[:, 4 * kk:4 * kk + 4, :]
            src = win_v[:, 4 * kk:4 * kk + 4, 0:768]
            op("pool", lambda e, dst=dst, src=src: e.dma_start(out=dst, in_=src), w=[dst], dma="wqkv%d" % kk)

    def load_wconv(gate):
        for kk in range(4):
            dst = WCONV[:, 2 * kk:2 * kk + 2, :]
            src = win_v[:, 2 * kk:2 * kk + 2, 768:IN_DIM]
            op("pool", lambda e, dst=dst, src=src: e.dma_start(out=dst, in_=src), r=[gate], w=[dst],
               dma="wconv%d" % kk)

    for blk_ in range(NRA):
        load_ada(blk_)

    xring = [av(R_XRES + s_ * 4 * K, F32, [D]) for s_ in range(3)]

    def load_x0(t):
        dst = xring[t % 3]
        op("sp", lambda e: e.dma_start(out=dst, in_=x_d[t * 128:(t + 1) * 128, :]), w=[dst], dma="x0_%d" % (t % 3))

    for t in range(3):
        load_x0(t)

    def mod_block(blk):
        b = 4 + blk % 2
        ps = PSF[0:2, b, 0:256]
        slot = ring_ada[blk % NRA]
        for k in range(8):
            op("pe", lambda e, ps=ps, k=k, slot=slot: e.matmul(ps, lhsT=sc_bf[:, k, :], rhs=slot[:, k, :],
                                                              start=(k == 0), stop=(k == 7)),
               r=[sc_bf, slot[:, k, :]], w=[ps])
        cols = slice(blk * 256, (blk + 1) * 256)
        ms = mstage[blk % 2]
        bsl = bada_sb[:, cols]
        op("dve", lambda e: e.tensor_tensor(out=ms, in0=ps, in1=bsl, op=ALU.add), r=[ps, bsl], w=[ms])
        sec = blk // 4
        if sec in (1, 4):
            ncol = 0 if sec == 1 else 1
            nsl = nm_sb[:, ncol * D + (blk % 4) * 256: ncol * D + (blk % 4 + 1) * 256]
            op("dve", lambda e: e.scalar_tensor_tensor(out=ms, in0=ms, scalar=1.0, in1=nsl,
                                                       op0=ALU.add, op1=ALU.mult), r=[ms, nsl], w=[ms])
        op("sp", lambda e: e.dma_start(out=scr_d[:, cols], in_=ms), r=[ms], w=[scr_d], dma="scrw%d" % (blk % 2))
        if blk + NRA < NBLK and not (4 <= blk < 8):
            load_ada(blk + NRA)

    for blk in range(8):
        mod_block(blk)
        if blk == 3:
            load_wqkv()

    def load_modb(dst, cond, col, stream):
        src = scr_d[cond:cond + 1, col * D:(col + 1) * D].broadcast_to([128, D])
        op("sp", lambda e: e.dma_start(out=dst, in_=src), r=[scr_d], w=[dst], dma=stream)

    gam1_b = [av(R_XRES + 12 * K + c * 8 * K, F32, [D]) for c in range(2)]
    sh1_b = [av(R_XRES + 16 * K + c * 8 * K, F32, [D]) for c in range(2)]
    for c in range(2):
        load_modb(gam1_b[c], c, 1, "g1b%d" % c)
        load_modb(sh1_b[c], c, 0, "s1b%d" % c)

    HT = av(R_HT, BF16, [8, NTOK])
    tmp0 = [av(R_XRES + 28 * K + s_ * 4 * K, F32, [D]) for s_ in range(4)]
    hb0 = [av(R_XRES + 44 * K + s_ * 2 * K, BF16, [D]) for s_ in range(2)]

    def norm_stats(xsrc, tmp, sscol):
        ssa = ss_c[:, sscol:sscol + 1]
        sda = sd_c[:, sscol:sscol + 1]
        op("act", lambda e: e.activation(out=tmp, in_=xsrc, func=AF.Square, accum_out=ssa),
           r=[xsrc], w=[tmp, ssa])
        op("act", lambda e: e.activation(out=sda, in_=ssa, func=AF.Sqrt, bias=eps_c[:, 0:1], scale=1.0 / D),
           r=[ssa, eps_c], w=[sda])

    def norm_apply(xsrc, gam_b, sh_b, tmp, hb, sscol, add_eng="pool"):
        sda = sd_c[:, sscol:sscol + 1]
        rsa = rs_c[:, sscol:sscol + 1]
        op("dve", lambda e: e.reciprocal(out=rsa, in_=sda), r=[sda], w=[rsa])
        op("dve", lambda e: e.scalar_tensor_tensor(out=tmp, in0=xsrc, scalar=rsa, in1=gam_b,
                                                   op0=ALU.mult, op1=ALU.mult),
           r=[xsrc, rsa, gam_b], w=[tmp])
        op(add_eng, lambda e: e.tensor_tensor(out=hb, in0=tmp, in1=sh_b, op=ALU.add), r=[tmp, sh_b], w=[hb])

    def transpose_to(hb, nchunk, dst3):
        bb = bbank()
        psb = PSF[:, bb, :].bitcast(BF16)
        for k in range(nchunk):
            pso = psb[:, k * 128:(k + 1) * 128]
            src = hb[:, k * 128:(k + 1) * 128]
            op("pe", lambda e, pso=pso, src=src: e.transpose(out=pso, in_=src, identity=ident_bf),
               r=[src, ident_bf], w=[pso])
        srcp = psb[:, 0:nchunk * 128].rearrange("p (k c) -> p k c", k=nchunk)
        op("act", lambda e: e.activation(out=dst3, in_=srcp, func=AF.Copy), r=[srcp], w=[dst3])

    xring = [av(R_XRES + s_ * 4 * K, F32, [D]) for s_ in range(3)]

    def p0_s0(t):
        norm_stats(xring[t % 3], tmp0[t % 4], t)

    def p0_s1(t):
        c = 0 if t < 8 else 1
        norm_apply(xring[t % 3], gam1_b[c], sh1_b[c], tmp0[t % 4], hb0[t % 2], t,
                   add_eng="dve")
        if t + 3 < NT:
            load_x0(t + 3)

    def p0_s2(t):
        transpose_to(hb0[t % 2], 8, HT[:, :, t * 128:(t + 1) * 128])

    skew(NT, [p0_s0, p0_s1, p0_s2])
    gate_ap = HT[:, :, (NT - 1) * 128:NT * 128]
    for blk_ in range(8, 8 + NRA):
        s__ = blk_ % NRA
        dst_ = ring_ada[s__]
        src_ = wada_v[:, :, blk_ * 256:(blk_ + 1) * 256]
        op("pool", lambda e, dst_=dst_, src_=src_: e.dma_start(out=dst_, in_=src_), r=[gate_ap], w=[dst_],
           dma="ada%d" % s__)
    load_wconv(gate_ap)

    MT = av(R_MT, BF16, [8, NTOK])
    QT = av(R_XRES, BF16, [4, NTOK])
    KT = av(R_XRES + 12 * K, BF16, [2, 1792])
    VO = av(R_XRES + 19 * K, BF16, [14, 2, 65])
    PT = [av(R_XRES + 23 * K + s_ * 2 * K, BF16, [2, 512]) for s_ in range(4)]
    attn_tm = av(R_XRES + 31 * K, F32, [4, 512])
    T0 = R_TEMP
    sq_sb = [av(T0 + s_ * 2 * K, F32, [512]) for s_ in range(2)]
    qn_sb = av(T0 + 4 * K, F32, [512])
    qg_sb = av(T0 + 6 * K, F32, [512])
    rA = av(T0 + 8 * K, F32, [256])
    rB = av(T0 + 9 * K, F32, [256])
    qr_bf = [av(T0 + 10 * K + s_ * K, BF16, [512]) for s_ in range(2)]
    kn_sb = av(T0 + 12 * K, F32, [128])
    kg_sb = [av(T0 + 12 * K + 512 + s_ * 512, F32, [128]) for s_ in range(2)]
    kf_sb = av(T0 + 14 * K, F32, [128])
    kd_bf = [av(T0 + 14 * K + 512 + s_ * 512, BF16, [256]) for s_ in range(2)]
    vs_sb = [av(T0 + 16 * K + s_ * 512, F32, [128]) for s_ in range(2)]
    ck_sb = av(T0 + 17 * K, F32, [2, 128])
    cv_sb = av(T0 + 18 * K, F32, [2, 128])
    aon_b = av(T0 + 19 * K, F32, [512])
    an_bf = [av(T0 + 21 * K + s_ * K, BF16, [512]) for s_ in range(2)]
    rsd = av(T0 + 23 * K, F32, [512])
    rsr = av(T0 + 25 * K, F32, [512])

    op("sp", lambda e: e.dma_start(out=ck_sb, in_=ck_d.rearrange("(t p) d -> p t d", p=128)), w=[ck_sb], dma="ck")
    op("sp", lambda e: e.dma_start(out=cv_sb, in_=cv_d.rearrange("(t p) d -> p t d", p=128)), w=[cv_sb], dma="cv")
    vo_ones = cap(VO, [(130, 14), (65, 2), (1, 1)], off=64)
    op("dve", lambda e: e.memset(vo_ones, 1.0), w=[VO])

    def k_dup(src_f32, kd, eng="dve"):
        dsto = cap(kd, [(128, 2), (64, 2), (1, 64)])
        srci = cap(src_f32, [(64, 2), (0, 2), (1, 64)])
        op(eng, lambda e: e.tensor_copy(out=dsto, in_=srci), r=[src_f32], w=[kd])

    def v_to_VO(src, vt):
        dst = VO[:, vt, :, 0:64]
        srcv = src.rearrange("p (k c) -> p k c", k=2)
        op("dve", lambda e: e.tensor_copy(out=dst, in_=srcv), r=[src], w=[dst])

    for tt in range(2):
        k_dup(ck_sb[:, tt, :], kd_bf[tt % 2])
        transpose_to(kd_bf[tt % 2], 2, KT[:, :, tt * 128:(tt + 1) * 128])
        v_to_VO(cv_sb[:, tt, :], tt)

    ropeT = {}
    for qi, (nm_, gsrc) in enumerate((("q", gq_b), ("k", gk_b))):
        for ti, (tab, goff) in enumerate(((cos_t, 0), (sin_t, 16), (sin_t, 0), (cos_t, 16))):
            dstT = av(T0 + 19 * K + (qi * 4 + ti) * K, F32, [8, 32])
            ropeT[(nm_, ti)] = dstT
            tv = cap(tab, [(32, 8), (16, 2), (1, 16)])
            gv = cap(gsrc, [(0, 8), (32, 2), (1, 16)], off=goff)
            ov = cap(dstT, [(32, 8), (16, 2), (1, 16)])
            op("dve", lambda e, tv=tv, gv=gv, ov=ov: e.tensor_tensor(out=ov, in0=tv, in1=gv, op=ALU.mult),
               r=[tab, gsrc], w=[dstT])
    rAk = av(T0 + 32000, F32, [64])
    rBk = av(T0 + 32256, F32, [64])
    sq_all = av(T0, F32, [640])
    s10 = cap(ss8, [(1, 10)])
    d10 = cap(sd8, [(1, 10)])
    r10 = cap(r8, [(1, 10)])

    def r10_col(i):
        return cap(r8, [(1, 1)], off=i)

    def rope_ops(src, dst, nh, tile, rA_, rB_, which):
        n = nh * 32
        x1 = cap(src, [(64, nh), (32, 2), (1, 16)])
        x2 = cap(src, [(64, nh), (32, 2), (1, 16)], off=16)
        o1 = cap(dst, [(64, nh), (32, 2), (1, 16)])
        o2 = cap(dst, [(64, nh), (32, 2), (1, 16)], off=16)
        tb = [cap(ropeT[(which, ti)][:, tile, :], [(0, nh), (16, 2), (1, 16)]) for ti in range(4)]
        tt_ = [ropeT[(which, ti)] for ti in range(4)]
        a = cap(rA_, [(32, nh), (16, 2), (1, 16)])
        b = cap(rB_, [(32, nh), (16, 2), (1, 16)])
        ra = rA_[:, 0:n]
        rb = rB_[:, 0:n]
        return [
            lambda: op("dve", lambda e: e.tensor_tensor(out=a, in0=x1, in1=tb[0], op=ALU.mult), r=[src, tt_[0]], w=[ra]),
            lambda: op("dve", lambda e: e.tensor_tensor(out=b, in0=x2, in1=tb[1], op=ALU.mult), r=[src, tt_[1]], w=[rb]),
            lambda: op("dve", lambda e: e.tensor_tensor(out=o1, in0=a, in1=b, op=ALU.subtract), r=[ra, rb], w=[dst]),
            lambda: op("dve", lambda e: e.tensor_tensor(out=a, in0=x1, in1=tb[2], op=ALU.mult), r=[src, tt_[2]], w=[ra]),
            lambda: op("dve", lambda e: e.tensor_tensor(out=b, in0=x2, in1=tb[3], op=ALU.mult), r=[src, tt_[3]], w=[rb]),
            lambda: op("dve", lambda e: e.tensor_tensor(out=o2, in0=a, in1=b, op=ALU.add), r=[ra, rb], w=[dst]),
        ]

    tm_banks = {}
    late_blk = [8]
    KENG = "dve"

    def p1_s0(t):
        tok = slice(t * 128, (t + 1) * 128)
        bq = fbank()
        bkv = fbank()
        tm_banks[t] = (bq, bkv)
        qps = PSF[:, bq, :]
        kvps = PSF[:, bkv, 0:256]
        for k in range(8):
            op("pe", lambda e, k=k: e.matmul(qps, lhsT=HT[:, k, tok], rhs=WQKV[:, k, 0:512],
                                             start=(k == 0), stop=(k == 7)),
               r=[HT[:, k, tok], WQKV[:, k, 0:512]], w=[qps])
        for k in range(8):
            op("pe", lambda e, k=k: e.matmul(kvps, lhsT=HT[:, k, tok], rhs=WQKV[:, k, 512:768],
                                             start=(k == 0), stop=(k == 7)),
               r=[HT[:, k, tok], WQKV[:, k, 512:768]], w=[kvps])
        if t >= 4:
            for _ in range(2):
                if late_blk[0] < NBLK:
                    mod_block(late_blk[0])
                    late_blk[0] += 1

    def p1_s1(t):
        sample = t < 8
        bq, bkv = tm_banks[t]
        qps = PSF[:, bq, :]
        kps = PSF[:, bkv, 0:128]
        vps = PSF[:, bkv, 128:256]
        op("act", lambda e: e.activation(out=sq_all[:, 0:512], in_=qps, func=AF.Square), r=[qps], w=[sq_all[:, 0:512]])
        op("act", lambda e: e.activation(out=sq_all[:, 512:640], in_=kps, func=AF.Square),
           r=[kps], w=[sq_all[:, 512:640]])
        op("dve", lambda e: e.tensor_reduce(out=s10, in_=sq_all.rearrange("p (h d) -> p h d", h=10),
                                            axis=AX.X, op=ALU.add), r=[sq_all], w=[s10])
        op("act", lambda e: e.activation(out=d10, in_=s10, func=AF.Sqrt, bias=eps_c[:, 0:1], scale=1.0 / 64),
           r=[s10, eps_c], w=[d10])
        op("dve", lambda e: e.reciprocal(out=r10, in_=d10), r=[d10], w=[r10])
        rb8 = cap(r10, [(1, 8), (0, 64)])
        rb2 = cap(r10, [(1, 2), (0, 64)], off=8)
        gqv = cap(gq_b, [(0, 8), (1, 64)])
        gkv = cap(gk_b, [(0, 2), (1, 64)])
        qr = qr_bf[t % 2]
        kg = kg_sb[t % 2]
        vs = vs_sb[t % 2]
        v3 = lambda x, h: x.rearrange("p (h d) -> p h d", h=h)
        qch = [lambda: op("dve", lambda e: e.tensor_tensor(out=v3(qn_sb, 8), in0=v3(qps, 8), in1=rb8, op=ALU.mult),
                          r=[qps, r10], w=[qn_sb])]
        kch = []
        for hk in range(2):
            ko = kn_sb[:, hk * 64:(hk + 1) * 64]
            ki = kps[:, hk * 64:(hk + 1) * 64]
            rk = r10_col(8 + hk)
            kch.append(lambda ko=ko, ki=ki, rk=rk: op("act", lambda e: e.activation(out=ko, in_=ki, func=AF.Copy, scale=rk),
                                                      r=[ki, rk], w=[ko]))
        if sample:
            vdst = VO[:, 2 + t, :, 0:64]
            kch.append(lambda: op("act", lambda e: e.activation(out=vdst, in_=vps.rearrange("p (k c) -> p k c", k=2),
                                                                func=AF.Copy), r=[vps], w=[vdst]))
            qch += rope_ops(qn_sb, qr, 8, t, rA, rB, "q")
            kch += rope_ops(kn_sb, kf_sb, 2, t, rAk, rBk, "k")
            kch.append(lambda: k_dup(kf_sb, kd_bf[t % 2]))
        else:
            pt = t - 8
            vdst = VO[:, 10 + pt, :, 0:64]
            qch.append(lambda: op("dve", lambda e: e.tensor_tensor(out=v3(qr, 8), in0=v3(qn_sb, 8), in1=gqv,
                                                                 op=ALU.mult), r=[qn_sb, gq_b], w=[qr]))
            kch.append(lambda: op("dve", lambda e: e.tensor_tensor(out=v3(kg, 2), in0=v3(kn_sb, 2), in1=gkv,
                                                                 op=ALU.mult), r=[kn_sb, gk_b], w=[kg]))
            kch.append(lambda: op("sp", lambda e: e.dma_start(out=newk_d[pt * 128:(pt + 1) * 128, :], in_=kg),
                                  r=[kg], dma="nk%d" % (t % 2), final=True))
            kch.append(lambda: op("act", lambda e: e.activation(out=vs, in_=vps, func=AF.Copy), r=[vps], w=[vs]))
            kch.append(lambda: op("sp", lambda e: e.dma_start(out=newv_d[pt * 128:(pt + 1) * 128, :], in_=vs),
                                  r=[vs], dma="nv%d" % (t % 2), final=True))
            kch.append(lambda: k_dup(kg, kd_bf[t % 2]))
            kch.append(lambda: op("act", lambda e: e.activation(out=vdst, in_=vps.rearrange("p (k c) -> p k c", k=2),
                                                                func=AF.Copy), r=[vps], w=[vdst]))
        for ii in range(max(len(qch), len(kch))):
            if ii < len(qch):
                qch[ii]()
            if ii < len(kch):
                kch[ii]()

    def p1_s2(t):
        tok = slice(t * 128, (t + 1) * 128)
        transpose_to(qr_bf[t % 2], 4, QT[:, :, tok])
        keypos = 256 + t * 128 if t < 8 else 1280 + (t - 8) * 128
        transpose_to(kd_bf[t % 2], 2, KT[:, :, keypos:keypos + 128])

    skew(NT, [p1_s0, p1_s1, p1_s2])
    op("sp", lambda e: e.dma_start(out=aon_b, in_=aon_d.broadcast_to([128, 512])), w=[aon_b], dma="aon")
    while late_blk[0] < NBLK:
        mod_block(late_blk[0])
        late_blk[0] += 1

    CB = R_WOUT
    u_sb = [av(CB, F32, [1024]), av(R_MT, F32, [1024])]
    gb_sb = [av(CB + 4 * K, F32, [1024]), av(R_MT + 4 * K, F32, [1024])]
    t_pad = [av(CB + 8 * K, F32, [1032]), av(T0 + 27 * K, F32, [1032])]
    yall = av(CB + 8 * K + 4128, F32, [4, 1024])
    ysq = [av(T0 + 14 * K, BF16, [1024]), av(T0 + 4 * K, BF16, [1024])]
    yall_g = [yall, av(T0 + 6 * K, F32, [4, 512])]
    assert 8 * K + 4128 + 16 * K + 2 * K <= 32 * K

    items = []
    for grp in range(2):
        for c in range(4):
            items.append((grp, c))

    def grp_def(grp):
        if grp == 0:
            return 0, 1, 1024, [0, 1]
        return 1024, 2, 256, [2]

    def fm_s0(i):
        grp, c = items[i]
        tok0, seqs, L, blocks = grp_def(grp)
        padw = L + 2
        par = c % 2
        if c < 2:
            pads = cap(t_pad[par], [(padw, seqs), (L + 1, 2)])
            op("dve", lambda e: e.memset(pads, 0.0), w=[t_pad[par]])
        for bi, blk in enumerate(blocks):
            btok = slice(blk * 512, (blk + 1) * 512)
            loc = slice(bi * 512, (bi + 1) * 512)
            pss = []
            for part in range(3):
                col = part * 512 + c * 128
                b = fbank()
                ps = PSF[:, b, :]
                pss.append(ps)
                for k in range(8):
                    op("pe", lambda e, ps=ps, k=k, col=col, btok=btok: e.matmul(
                        ps, lhsT=WCONV[:, k, col:col + 128], rhs=HT[:, k, btok], start=(k == 0), stop=(k == 7)),
                       r=[WCONV[:, k, col:col + 128], HT[:, k, btok]], w=[ps])
            gbp, gcp, up = pss
            ul = u_sb[par][:, loc]
            gl = gb_sb[par][:, loc]
            op("act", lambda e, up=up, ul=ul: e.activation(out=ul, in_=up, func=AF.Copy), r=[up], w=[ul])
            op("act", lambda e, gbp=gbp, gl=gl: e.activation(out=gl, in_=gbp, func=AF.Copy), r=[gbp], w=[gl])
            if seqs == 1:
                tdst = t_pad[par][:, 1 + bi * 512: 1 + (bi + 1) * 512]
                tin0 = gcp
                tin1 = ul
            else:
                tdst = cap(t_pad[par], [(padw, seqs), (1, L)], off=1)
                tin0 = gcp.rearrange("p (s l) -> p s l", s=seqs)
                tin1 = ul.rearrange("p (s l) -> p s l", s=seqs)
            op("dve", lambda e, tdst=tdst, tin0=tin0, tin1=tin1: e.tensor_tensor(
                out=tdst, in0=tin0, in1=tin1, op=ALU.mult), r=[gcp, ul], w=[t_pad[par]])

    def fm_s1(i):
        grp, c = items[i]
        tok0, seqs, L, blocks = grp_def(grp)
        ntok = seqs * L
        padw = L + 2
        par = c % 2
        accb = [6 + bi for bi in range(len(blocks))]
        tp_ = t_pad[par]
        yv = cap(u_sb[par], [(L, seqs), (1, L)])
        t0 = cap(tp_, [(padw, seqs), (1, L)], off=0)
        t1 = cap(tp_, [(padw, seqs), (1, L)], off=1)
        t2 = cap(tp_, [(padw, seqs), (1, L)], off=2)
        w0 = convw[:, c, 0:1]
        w1 = convw[:, c, 1:2]
        w2 = convw[:, c, 2:3]
        uall = u_sb[par][:, 0:ntok]
        gall = gb_sb[par][:, 0:ntok]
        op("dve", lambda e: e.tensor_scalar(out=yv, in0=t0, scalar1=w0, scalar2=None, op0=ALU.mult),
           r=[tp_, convw], w=[uall])
        op("dve", lambda e: e.scalar_tensor_tensor(out=yv, in0=t1, scalar=w1, in1=yv, op0=ALU.mult, op1=ALU.add),
           r=[tp_, convw, uall], w=[uall])
        op("dve", lambda e: e.scalar_tensor_tensor(out=yv, in0=t2, scalar=w2, in1=yv, op0=ALU.mult, op1=ALU.add),
           r=[tp_, convw, uall], w=[uall])
        ydst = yall_g[grp][:, c, 0:ntok]
        op("dve", lambda e: e.tensor_tensor(out=ydst, in0=uall, in1=gall, op=ALU.mult), r=[uall, gall], w=[ydst])

    def fm_s1b(i):
        grp, c = items[i]
        tok0, seqs, L, blocks = grp_def(grp)
        ntok = seqs * L
        ydst = yall_g[grp][:, c, 0:ntok]
        ysl = ysq[i % 2][:, 0:ntok]
        op("act", lambda e: e.activation(out=ysl, in_=ydst, func=AF.Square), r=[ydst], w=[ysl])

    def fm_s2(i):
        grp, c = items[i]
        tok0, seqs, L, blocks = grp_def(grp)
        accb = [6 + bi for bi in range(len(blocks))]
        for bi in range(len(blocks)):
            aps = PSF[:, accb[bi], :]
            rh = ysq[i % 2][:, bi * 512:(bi + 1) * 512]
            op("pe", lambda e, aps=aps, rh=rh: e.matmul(aps, lhsT=ones_bf, rhs=rh, start=(c == 0), stop=(c == 3)),
               r=[ones_bf, rh], w=[aps])
        if c == 3:
            for bi, blk in enumerate(blocks):
                aps = PSF[:, accb[bi], :]
                op("act", lambda e, aps=aps: e.activation(out=rsd, in_=aps, func=AF.Sqrt, bias=eps_c[:, 0:1],
                                                          scale=1.0 / 512), r=[aps, eps_c], w=[rsd])
                op("dve", lambda e: e.reciprocal(out=rsr, in_=rsd), r=[rsd], w=[rsr])
                for cc_ in range(4):
                    dst = MT[:, 4 + cc_, blk * 512:(blk + 1) * 512]
                    ysrc = yall_g[grp][:, cc_, bi * 512:(bi + 1) * 512]
                    cc = con[:, cc_:cc_ + 1]
                    op("dve", lambda e, dst=dst, ysrc=ysrc, cc=cc: e.scalar_tensor_tensor(
                        out=dst, in0=ysrc, scalar=cc, in1=rsr, op0=ALU.mult, op1=ALU.mult),
                       r=[ysrc, con, rsr], w=[dst])

    nrot[0] = 6
    nI = len(items)
    fm_tail = skew(nI, [fm_s0, fm_s1, fm_s1b, fm_s2],
                   defer={(2, nI - 2), (2, nI - 1), (3, nI - 2), (3, nI - 1)})

    g1_b = [av(R_WIN + c * 12 * K, F32, [D]) for c in range(2)]
    gam2_b = [av(R_WIN + 4 * K + c * 12 * K, F32, [D]) for c in range(2)]
    sh2_b = [av(R_WIN + 8 * K + c * 12 * K, F32, [D]) for c in range(2)]
    for c in range(2):
        load_modb(g1_b[c], c, 2, "m_g1%d" % c)
        load_modb(gam2_b[c], c, 4, "m_gam2%d" % c)
        load_modb(sh2_b[c], c, 3, "m_sh2%d" % c)
    WOUT = av(R_WOUT, BF16, [8, D])
    wout_v = wout_d.rearrange("(k p) n -> p k n", p=128)
    op("pool", lambda e: e.dma_start(out=WOUT, in_=wout_v), w=[WOUT], dma="wout")
    ring_gu = [av(R_RING + s_ * 4 * K, BF16, [2, 8, 128]) for s_ in range(4)]
    wgu_v = wgu_d.rearrange("(k p) n -> p k n", p=128)

    def load_gu(f):
        s_ = f % 4
        dst = ring_gu[s_]
        op("pool", lambda e: e.dma_start(out=dst[:, 0], in_=wgu_v[:, :, f * 128:(f + 1) * 128]),
           w=[dst[:, 0]], dma="gua%d" % s_)
        op("pool", lambda e: e.dma_start(out=dst[:, 1],
                                         in_=wgu_v[:, :, DFF + f * 128:DFF + (f + 1) * 128]),
           w=[dst[:, 1]], dma="gub%d" % s_)

    for f_ in range(4):
        load_gu(f_)

    ptc = [0]
    pending = [(False, (lambda: None), -1)] * 5 + [(True, th, -1) for th in fm_tail]
    qbi = [0]
    attn_bufs = [attn_tm, av(R_XRES + 39 * K, F32, [4, 512])]
    seqdefs = [
        (0, 1024, 0, 10, 0),
        (1024, 256, 1280, 2, 10),
        (1280, 256, 1536, 2, 12),
    ]
    for (q0, nq, kb, nkc, vt0) in seqdefs:
        Nq = min(nq, 512)
        ntile = Nq // 128
        for qb in range(nq // Nq):
            qtok0 = q0 + qb * Nq
            attn_cur = attn_bufs[qbi[0] % 2]
            while pending and pending[0][2] <= qbi[0] - 2:
                pending.pop(0)[1]()
            jc = [(j, c) for j in range(4) for c in range(nkc)]
            pts = {}

            def at_s0(i, jc=jc, pts=pts, qtok0=qtok0, Nq=Nq, kb=kb):
                j, c = jc[i]
                kv = j // 2
                pb = (i % 2) * 2
                for hh in range(2):
                    pr = slice(hh * 64, hh * 64 + 64)
                    sps = PSF[:, pb + hh, 0:Nq]
                    lk = KT[pr, kv, kb + c * 128: kb + (c + 1) * 128]
                    rq = QT[pr, j, qtok0:qtok0 + Nq]
                    op("pe", lambda e, sps=sps, lk=lk, rq=rq: e.matmul(sps, lhsT=lk, rhs=rq, start=True, stop=True),
                       r=[lk, rq], w=[sps])
                pt = PT[ptc[0] % 4][:, :, 0:Nq]
                ptc[0] += 1
                pts[i] = pt
                for hh in range(2):
                    sph = PSF[:, pb + hh, 0:Nq]
                    pth = pt[:, hh, :]
                    op("act", lambda e, sph=sph, pth=pth: e.activation(out=pth, in_=sph, func=AF.Exp, scale=0.125),
                       r=[sph], w=[pth])

            def at_s1(i, jc=jc, pts=pts, ntile=ntile, nkc=nkc, vt0=vt0, attn_cur=attn_cur):
                j, c = jc[i]
                kv = j // 2
                pt = pts[i]
                vo = VO[:, vt0 + c, kv, :]
                for hh in range(2):
                    accbank = 4 + (j % 2) * 2 + hh
                    acc = PSF[:, accbank, 0:4 * 65].rearrange("p (t d) -> p t d", t=4)
                    for tq in range(ntile):
                        at = acc[:, tq, :]
                        lp = pt[:, hh, tq * 128:(tq + 1) * 128]
                        op("pe", lambda e, at=at, lp=lp, tq=tq: e.matmul(
                            at, lhsT=lp, rhs=vo, start=(c == 0 and tq == 0), stop=(c == nkc - 1),
                            skip_group_check=True), r=[lp, vo], w=[at])
                if c == nkc - 1:
                    for hh in range(2):
                        h = 2 * j + hh
                        accbank = 4 + (j % 2) * 2 + hh
                        acc = PSF[:, accbank, 0:4 * 65].rearrange("p (t d) -> p t d", t=4)
                        rd = rden[:, hh, 0:ntile]
                        den = cap(acc, [(65, ntile)], off=64)
                        op("dve", lambda e, rd=rd, den=den: e.reciprocal(out=rd, in_=den), r=[acc], w=[rd])
                        dst = attn_cur[:, 0:ntile, h * 64:(h + 1) * 64]
                        num = acc[:, 0:ntile, 0:64]
                        rdb = cap(rd, [(1, ntile), (0, 64)])
                        op("dve", lambda e, dst=dst, num=num, rdb=rdb: e.tensor_tensor(
                            out=dst, in0=num, in1=rdb, op=ALU.mult), r=[acc, rd], w=[dst])

            ns_ = 2
            for step in range(len(jc) + ns_ - 1):
                for s_ in range(ns_):
                    it = step - s_
                    if 0 <= it < len(jc):
                        (at_s0, at_s1)[s_](it)
                if pending:
                    jcur = jc[min(step, len(jc) - 1)][0]
                    if not (pending[0][0] and jcur % 2 == 1):
                        pending.pop(0)[1]()
            abuf = attn_cur
            cols = slice(12, 12 + ntile)

            def mk_sq(tq, abuf=abuf):
                src = abuf[:, tq, :]
                ssa = ss_c[:, 12 + tq: 13 + tq]
                sq = sq_sb[tq % 2]
                return lambda: op("act", lambda e: e.activation(out=sq, in_=src, func=AF.Square, accum_out=ssa),
                                  r=[src], w=[sq, ssa])

            def mk_rs(cols=cols):
                def f():
                    op("act", lambda e: e.activation(out=sd_c[:, cols], in_=ss_c[:, cols], func=AF.Ln,
                                                     bias=eps_c[:, 0:1], scale=1.0 / 512),
                       r=[ss_c[:, cols], eps_c], w=[sd_c[:, cols]])
                    op("act", lambda e: e.activation(out=rs_c[:, cols], in_=sd_c[:, cols], func=AF.Exp, scale=-0.5),
                       r=[sd_c[:, cols]], w=[rs_c[:, cols]])
                return f

            def mk_tr(tq, abuf=abuf, qtok0=qtok0):
                def f():
                    tglob = (qtok0 // 128) + tq
                    src = abuf[:, tq, :]
                    rsa = rs_c[:, 12 + tq: 13 + tq]
                    an = an_bf[tq % 2]
                    op("dve", lambda e: e.scalar_tensor_tensor(out=an, in0=src, scalar=rsa, in1=aon_b,
                                                               op0=ALU.mult, op1=ALU.mult),
                       r=[src, rsa, aon_b], w=[an])
                    transpose_to(an, 4, MT[:, 0:4, tglob * 128:(tglob + 1) * 128])
                return f

            for tq in range(ntile):
                pending.append((False, mk_sq(tq), qbi[0]))
            pending.append((False, mk_rs(), qbi[0]))
            for tq in range(ntile):
                pending.append((True, mk_tr(tq), qbi[0]))
            qbi[0] += 1
    while pending:
        pending.pop(0)[1]()

    XRES = av(R_XRES, F32, [NT, D])
    x3ring = [av(R_TEMP + s_ * 4 * K, F32, [D]) for s_ in range(3)]
    tmp3 = [av(R_TEMP + 12 * K + s_ * 4 * K, F32, [D]) for s_ in range(4)]
    hb3 = [av(R_TEMP + 28 * K + s_ * 2 * K, BF16, [D]) for s_ in range(2)]
    g1_b = [av(R_WIN + c * 12 * K, F32, [D]) for c in range(2)]
    gam2_b = [av(R_WIN + 4 * K + c * 12 * K, F32, [D]) for c in range(2)]
    sh2_b = [av(R_WIN + 8 * K + c * 12 * K, F32, [D]) for c in range(2)]
    g2_b = [av(R_WOUT + 8 * K + c * 4 * K, F32, [D]) for c in range(2)]

    def load_x3(t):
        dst = x3ring[t % 3]
        op("sp", lambda e: e.dma_start(out=dst, in_=x_d[t * 128:(t + 1) * 128, :]), w=[dst], dma="x3_%d" % (t % 3))

    for t in range(3):
        load_x3(t)
    p3_banks = {}

    def p3_s0(t):
        tok = slice(t * 128, (t + 1) * 128)
        bs = []
        for hh in range(2):
            b = fbank()
            bs.append(b)
            ps = PSF[:, b, :]
            for k in range(8):
                op("pe", lambda e, ps=ps, k=k, hh=hh: e.matmul(
                    ps, lhsT=MT[:, k, tok], rhs=WOUT[:, k, hh * 512:(hh + 1) * 512], start=(k == 0), stop=(k == 7)),
                   r=[MT[:, k, tok], WOUT[:, k, hh * 512:(hh + 1) * 512]], w=[ps])
        p3_banks[t] = bs

    def p3_s1(t):
        c = 0 if t < 8 else 1
        xs = x3ring[t % 3]
        for hh in range(2):
            ps = PSF[:, p3_banks[t][hh], :]
            cs = slice(hh * 512, (hh + 1) * 512)
            tm = tmp3[t % 4][:, cs]
            gsl = g1_b[c][:, cs]
            op("dve", lambda e, tm=tm, ps=ps, gsl=gsl: e.tensor_tensor(out=tm, in0=ps, in1=gsl, op=ALU.mult),
               r=[ps, gsl], w=[tm])
            xo = XRES[:, t, cs]
            xsl = xs[:, cs]
            op("dve", lambda e, xo=xo, tm=tm, xsl=xsl: e.tensor_tensor(out=xo, in0=tm, in1=xsl, op=ALU.add),
               r=[tm, xsl], w=[xo])
        if t + 3 < NT:
            load_x3(t + 3)
        norm_stats(XRES[:, t, :], tmp3[t % 4], t)

    def p3_s1b(t):
        c = 0 if t < 8 else 1
        norm_apply(XRES[:, t, :], gam2_b[c], sh2_b[c], tmp3[t % 4], hb3[t % 2], t,
                   add_eng="dve")

    def p3_s2(t):
        transpose_to(hb3[t % 2], 8, HT[:, :, t * 128:(t + 1) * 128])

    p3_tail = skew(NT, [p3_s0, p3_s1, p3_s1b, p3_s2], defer={(3, NT - 2), (3, NT - 1)})

    for c in range(2):
        load_modb(g2_b[c], c, 5, "m_g2%d" % c)
    ACTT = av(R_WIN, BF16, [11, NTOK])
    WDN = av(R_MT, BF16, [11, D])
    wdn_v = wdn_d.rearrange("(f p) n -> p f n", p=128)
    sg = [av(R_WOUT + s * 2 * K, F32, [512]) for s in range(2)]
    tm4 = [av(R_WOUT + 4 * K + s * 2 * K, F32, [512]) for s in range(2)]
    cnt = [0]
    for half in range(2):
        fl0 = half * 11
        wsrc = wdn_v[:, fl0:fl0 + 11, :]
        op("pool", lambda e, wsrc=wsrc: e.dma_start(out=WDN, in_=wsrc), w=[WDN], dma="wdn")
        for fl in range(11):
            f = fl0 + fl
            slot = ring_gu[f % 4]
            for tb in range(3):
                if tb == 2 and p3_tail:
                    for th in p3_tail:
                        th()
                    p3_tail = []
                btok = slice(tb * 512, (tb + 1) * 512)
                bg = fbank()
                bu = fbank()
                gps = PSF[:, bg, :]
                ups = PSF[:, bu, :]
                for (ps, c0) in ((gps, 0), (ups, 1)):
                    for k in range(8):
                        op("pe", lambda e, ps=ps, k=k, c0=c0, btok=btok, slot=slot: e.matmul(
                            ps, lhsT=slot[:, c0, k, :], rhs=HT[:, k, btok], start=(k == 0), stop=(k == 7)),
                           r=[slot, HT[:, k, btok]], w=[ps])
                s_ = sg[cnt[0] % 2]
                cnt[0] += 1
                op("act", lambda e, s_=s_, gps=gps: e.activation(out=s_, in_=gps, func=AF.Silu), r=[gps], w=[s_])
                dst = ACTT[:, fl, btok]
                op("dve", lambda e, dst=dst, s_=s_, ups=ups: e.tensor_tensor(out=dst, in0=s_, in1=ups, op=ALU.mult),
                   r=[s_, ups], w=[dst])
            if f + 4 < NF:
                load_gu(f + 4)
        for t in range(NT):
            c = 0 if t < 8 else 1
            tok = slice(t * 128, (t + 1) * 128)
            for hh in range(2):
                b = fbank()
                ps = PSF[:, b, :]
                cs = slice(hh * 512, (hh + 1) * 512)
                for fl in range(11):
                    op("pe", lambda e, ps=ps, fl=fl, tok=tok, cs=cs: e.matmul(
                        ps, lhsT=ACTT[:, fl, tok], rhs=WDN[:, fl, cs], start=(fl == 0), stop=(fl == 10)),
                       r=[ACTT[:, fl, tok], WDN[:, fl, cs]], w=[ps])
                tm = tm4[cnt[0] % 2]
                cnt[0] += 1
                op("dve", lambda e, tm=tm, ps=ps, c=c, cs=cs: e.tensor_tensor(out=tm, in0=ps, in1=g2_b[c][:, cs],
                                                                             op=ALU.mult),
                   r=[ps, g2_b[c][:, cs]], w=[tm])
                xo = XRES[:, t, cs]
                op("dve", lambda e, xo=xo, tm=tm: e.tensor_tensor(out=xo, in0=xo, in1=tm, op=ALU.add),
                   r=[xo, tm], w=[xo])
            if half == 1:
                xt = XRES[:, t, :]
                op("sp", lambda e, xt=xt, t=t: e.dma_start(out=y_d[t * 128:(t + 1) * 128, :], in_=xt),
                   r=[xt], dma="yout%d" % (t % 4), final=True)

    P.emit()
    return nc


def _rope_tables():
    rows = 1024 // 64
    row = np.repeat(np.arange(rows, dtype=np.float64), 64)
    col = np.tile(np.arange(64, dtype=np.float64), rows)
    inv = 1.0 / (10000.0 ** (np.arange(16, dtype=np.float64) / 16.0))
    ang = np.stack([row[:, None] * inv, col[:, None] * inv], axis=1)
    return (np.cos(ang).astype(np.float32).reshape(1024, 32),
            np.sin(ang).astype(np.float32).reshape(1024, 32))


_CACHE = {}


def kernel(x_prompt, x_sample, c, cache_k, cache_v, c_ctx, norm_mix, norm_ffn, w_ada, b_ada,
           w_in, q_norm, k_norm, conv_w, attn_out_norm, conv_out_norm, w_out, w_gate_up, w_down):
    f = lambda a: np.ascontiguousarray(np.asarray(a, dtype=np.float32))
    x_prompt, x_sample, c, cache_k, cache_v, c_ctx = map(f, (x_prompt, x_sample, c, cache_k, cache_v, c_ctx))
    if "nc" not in _CACHE:
        _CACHE["nc"] = build_program()
    nc = _CACHE["nc"]
    cos, sin = _rope_tables()
    shared = {
        "norm_mix": f(norm_mix).reshape(1, D),
        "norm_ffn": f(norm_ffn).reshape(1, D),
        "w_ada": f(w_ada).reshape(D, 6 * D),
        "b_ada": f(b_ada).reshape(1, 6 * D),
        "w_in": f(w_in).reshape(D, IN_DIM),
        "q_norm": f(q_norm).reshape(1, 64),
        "k_norm": f(k_norm).reshape(1, 64),
        "convw_fm": f(np.transpose(f(conv_w).reshape(3, 4, 128), (2, 1, 0))),
        "attn_out_norm": f(attn_out_norm).reshape(1, 512),
        "con_fm": f(f(conv_out_norm).reshape(4, 128).T),
        "w_out": f(w_out).reshape(D, D),
        "w_gate_up": f(w_gate_up).reshape(D, 2 * DFF),
        "w_down": f(w_down).reshape(DFF, D),
        "ident": np.eye(128, dtype=np.float32),
        "rope_cos": cos,
        "rope_sin": sin,
    }
    in_maps = []
    for i in range(N_CORES):
        xa = np.concatenate([x_sample[i], x_prompt[2 * i], x_prompt[2 * i + 1]], axis=0)
        cond = np.stack([c[i], c_ctx], axis=1)
        condT = f(np.transpose(cond.reshape(8, 128, 2), (1, 0, 2)))
        m = dict(shared)
        m["x"] = f(xa)
        m["condT"] = condT
        m["cache_k"] = f(cache_k[i, 0].reshape(256, 128))
        m["cache_v"] = f(cache_v[i, 0].reshape(256, 128))
        in_maps.append(m)
    res = run_bass_kernel_spmd(nc, in_maps, core_ids=list(range(N_CORES)))
    y_prompt = np.empty((16, 256, D), np.float32)
    y_sample = np.empty((8, 1024, D), np.float32)
    new_k = np.empty((16, 1, 256, 2, 64), np.float32)
    new_v = np.empty((16, 1, 256, 2, 64), np.float32)
    for i in range(N_CORES):
        r = res.results[i]
        y = np.asarray(r["y"])
        y_sample[i] = y[0:1024]
        y_prompt[2 * i] = y[1024:1280]
        y_prompt[2 * i + 1] = y[1280:1536]
        nk = np.asarray(r["newk"]).reshape(2, 256, 2, 64)
        nv = np.asarray(r["newv"]).reshape(2, 256, 2, 64)
        new_k[2 * i, 0] = nk[0]
        new_k[2 * i + 1, 0] = nk[1]
        new_v[2 * i, 0] = nv[0]
        new_v[2 * i + 1, 0] = nv[1]
    return (y_prompt, y_sample, new_k, new_v)
```
